# Optimizing a Trainium2 kernel written in Bass

```python
import math
import jax, jax.numpy as jnp
from jax import lax
import numpy as np

D_MODEL = 1024
BATCH = 8
SEQ = 2048
DEPTH = 4
DEC_BATCH = 128
DEC_SEQ = 1
PAST_LEN = 16384
PAGE_SIZE = 128

GLA_WIDTH = D_MODEL // 2
N_HEADS_GLA = 4
HEAD_V = GLA_WIDTH // N_HEADS_GLA
HEAD_K = HEAD_V // 2
KEY_WIDTH = N_HEADS_GLA * HEAD_K
GATE_RANK = 16
GATE_TEMP = 16.0
GLA_CHUNK = 64
POOL_WIDTH = D_MODEL - GLA_WIDTH
POOL_WINDOWS = (2, 4, 8, 16)
N_POOL_GROUPS = len(POOL_WINDOWS)
POOL_GROUP = POOL_WIDTH // N_POOL_GROUPS
POOL_BUF = max(POOL_WINDOWS) - 1
D_FF = 4 * D_MODEL
IN_WIDTH = 2 * KEY_WIDTH + 2 * GLA_WIDTH + GATE_RANK + POOL_WIDTH
EPS = 1e-6

kernel_name = "hymba_gla_pool_decoder_step"


def rmsnorm(x, g):
    xf = x.astype(jnp.float32)
    y = xf * lax.rsqrt(jnp.mean(xf * xf, axis=-1, keepdims=True) + EPS)
    return (y * g.astype(jnp.float32)).astype(x.dtype)


def gla_chunked(q, k, v, log_a, s0):
    B, T, H, dk = q.shape
    dv = v.shape[-1]
    C = math.gcd(T, GLA_CHUNK)
    N = T // C
    f32 = jnp.float32
    rs = lambda t: t.reshape(B, N, C, H, t.shape[-1]).astype(f32)
    q, k, v, la = rs(q), rs(k), rs(v), rs(log_a)
    b = jnp.cumsum(la, axis=2)
    b_last = b[:, :, -1:]
    qt = q * jnp.exp(b) * (HEAD_K ** -0.5)
    kt = k * jnp.exp(-b)
    ke = k * jnp.exp(b_last - b)
    mask = jnp.tril(jnp.ones((C, C), dtype=bool))
    att = jnp.einsum('bnchk,bnshk->bnhcs', qt, kt)
    att = jnp.where(mask, att, 0.0)
    o_intra = jnp.einsum('bnhcs,bnshv->bnchv', att, v)
    decay = jnp.exp(b_last[:, :, 0])

    def step(S, xs):
        qn, ken, vn, dn = xs
        o = jnp.einsum('bchk,bhkv->bchv', qn, S)
        S = dn[..., None] * S + jnp.einsum('bchk,bchv->bhkv', ken, vn)
        return S, o

    mv = lambda t: jnp.moveaxis(t, 1, 0)
    s_fin, o_inter = lax.scan(step, s0.astype(f32), (mv(qt), mv(ke), mv(v), mv(decay)))
    o = o_intra + jnp.moveaxis(o_inter, 0, 1)
    return o.reshape(B, T, H, dv), s_fin


def pool_mix(u, prefix):
    ext = jnp.concatenate([prefix.astype(u.dtype), u], axis=1)
    B, L, Cw = ext.shape
    P = prefix.shape[1]
    T = u.shape[1]
    ef = ext.astype(jnp.float32)
    cs = jnp.concatenate([jnp.zeros((B, 1, Cw), jnp.float32), jnp.cumsum(ef, axis=1)], axis=1)
    w = jnp.repeat(jnp.array(POOL_WINDOWS, jnp.int32), POOL_GROUP)
    i = jnp.arange(P, L, dtype=jnp.int32)[:, None]
    lo = jnp.maximum(i - w[None, :] + 1, 0)
    s_lo = jnp.take_along_axis(cs, jnp.broadcast_to(lo[None], (B, T, Cw)), axis=1)
    s = cs[:, P + 1:L + 1] - s_lo
    count = (i + 1 - lo).astype(jnp.float32)
    out = s / count - ef[:, P:]
    return out, ext[:, L - POOL_BUF:]


def layer(x, s0, prefix, n1, w_in, w_gate, b_gate, gla_g, pool_w, pool_scale, w_out, n2, w_up, w_down):
    B, T, _ = x.shape
    h = rmsnorm(x, n1)
    z = h @ w_in
    c = np.cumsum([KEY_WIDTH, KEY_WIDTH, GLA_WIDTH, GLA_WIDTH, GATE_RANK])
    q, k, v, g, a_low, u = jnp.split(z, [int(t) for t in c], axis=-1)
    q = q.reshape(B, T, N_HEADS_GLA, HEAD_K)
    k = k.reshape(B, T, N_HEADS_GLA, HEAD_K)
    v = v.reshape(B, T, N_HEADS_GLA, HEAD_V)
    log_a = jax.nn.log_sigmoid((a_low @ w_gate + b_gate).astype(jnp.float32)) / GATE_TEMP
    log_a = log_a.reshape(B, T, N_HEADS_GLA, HEAD_K)
    o, s_new = gla_chunked(q, k, v, log_a, s0)
    o = o * lax.rsqrt(jnp.mean(o * o, axis=-1, keepdims=True) + EPS) * gla_g.astype(jnp.float32)
    o = o.reshape(B, T, GLA_WIDTH) * jax.nn.silu(g.astype(jnp.float32))
    p, buf = pool_mix(u, prefix)
    p = jnp.einsum('btgc,gcd->btgd', p.reshape(B, T, N_POOL_GROUPS, POOL_GROUP),
                   pool_w.astype(jnp.float32)).reshape(B, T, POOL_WIDTH)
    p = p * pool_scale.astype(jnp.float32)
    mix = jnp.concatenate([o, p], axis=-1).astype(x.dtype)
    x = x + mix @ w_out
    h2 = rmsnorm(x, n2)
    x = x + jnp.square(jax.nn.relu(h2 @ w_up)) @ w_down
    return x, s_new, buf


def setup_inputs(seed: int = 0) -> dict:
    key = jax.random.key(seed)
    ks = jax.random.split(key, 16)
    nrm = jax.random.normal
    f = jnp.float32
    return {
        "x_prompt": nrm(ks[0], (BATCH, SEQ, D_MODEL), f),
        "x_sample": nrm(ks[1], (DEC_BATCH, DEC_SEQ, D_MODEL), f),
        "state_gla": 0.5 * nrm(ks[2], (DEPTH, DEC_BATCH, N_HEADS_GLA, HEAD_K, HEAD_V), f),
        "state_pool": nrm(ks[3], (DEPTH, DEC_BATCH, POOL_BUF, POOL_WIDTH), f),
        "norm1_g": 1.0 + 0.02 * nrm(ks[4], (DEPTH, D_MODEL), f),
        "w_in": nrm(ks[5], (DEPTH, D_MODEL, IN_WIDTH), f) * D_MODEL ** -0.5,
        "w_gate": nrm(ks[6], (DEPTH, GATE_RANK, KEY_WIDTH), f) * GATE_RANK ** -0.5,
        "b_gate": 0.1 * nrm(ks[7], (DEPTH, KEY_WIDTH), f),
        "gla_norm_g": 1.0 + 0.02 * nrm(ks[8], (DEPTH, HEAD_V), f),
        "pool_w": nrm(ks[9], (DEPTH, N_POOL_GROUPS, POOL_GROUP, POOL_GROUP), f) * POOL_GROUP ** -0.5,
        "pool_scale": 1.0 + 0.02 * nrm(ks[10], (DEPTH, POOL_WIDTH), f),
        "w_out": nrm(ks[11], (DEPTH, D_MODEL, D_MODEL), f) * D_MODEL ** -0.5,
        "norm2_g": 1.0 + 0.02 * nrm(ks[12], (DEPTH, D_MODEL), f),
        "w_up": nrm(ks[13], (DEPTH, D_MODEL, D_FF), f) * D_MODEL ** -0.5,
        "w_down": nrm(ks[14], (DEPTH, D_FF, D_MODEL), f) * D_FF ** -0.5,
        "final_g": 1.0 + 0.02 * nrm(ks[15], (D_MODEL,), f),
    }


def reference(x_prompt, x_sample, state_gla, state_pool, norm1_g, w_in, w_gate, b_gate,
              gla_norm_g, pool_w, pool_scale, w_out, norm2_g, w_up, w_down, final_g):
    xp, xs = x_prompt, x_sample
    B = xp.shape[0]
    gla_p, pool_p, gla_s, pool_s = [], [], [], []
    for l in range(DEPTH):
        params = (norm1_g[l], w_in[l], w_gate[l], b_gate[l], gla_norm_g[l], pool_w[l],
                  pool_scale[l], w_out[l], norm2_g[l], w_up[l], w_down[l])
        s0 = jnp.zeros((B, N_HEADS_GLA, HEAD_K, HEAD_V), jnp.float32)
        pre0 = jnp.zeros((B, 0, POOL_WIDTH), xp.dtype)
        xp, sp, bp = layer(xp, s0, pre0, *params)
        xs, ss, bs = layer(xs, state_gla[l], state_pool[l], *params)
        gla_p.append(sp.astype(state_gla.dtype))
        pool_p.append(bp.astype(state_pool.dtype))
        gla_s.append(ss.astype(state_gla.dtype))
        pool_s.append(bs.astype(state_pool.dtype))
    y_prompt = rmsnorm(xp, final_g)
    y_sample = rmsnorm(xs, final_g)
    return (y_prompt, y_sample, jnp.stack(gla_p), jnp.stack(pool_p), jnp.stack(gla_s), jnp.stack(pool_s))
```

```python
import numpy as np
from contextlib import ExitStack
import concourse.bass as bass
import concourse.mybir as mybir
from concourse.bass_utils import run_bass_kernel_spmd

F32 = mybir.dt.float32
BF16 = mybir.dt.bfloat16
ALU = mybir.AluOpType
AF = mybir.ActivationFunctionType

QUEUES = ("pe", "act", "dve", "pool", "sp")

D = 1024
DEPTH = 4
NCORE = 8
SEQ = 2048
NS = 16
NTOK = SEQ + NS
IN_W = 2064
DFF = 4096
NT = 1040
GM = 256
EPS = 1e-6
POOL_W = (2, 4, 8, 16)
CQ, CK, CV, CG, CA, CU = 0, 256, 512, 1024, 1536, 1552
C_ID, C_TRI, C_ONE, C_INVC, C_EPS, C_WSEL, C_WIC, NCON = 0, 128, 256, 384, 399, 400, 432, 492
PV_L, PV_G1, PV_G2, PV_GG, PV_PS, PV_FIN, NPV = 21, 0, 8, 16, 17, 84, 92


class Op:
    __slots__ = ("q", "fn", "deps", "slot", "fill", "sig", "waits", "known", "signals", "idx")


class Fill:
    __slots__ = ("last",)


class Sched:
    def __init__(self, same_engine_sync=True):
        self.ops = []
        self.last_writer = {}
        self.readers = {}
        self.same_engine_sync = same_engine_sync
        self.slots = []
        self.slot_fill = {}

    def add(self, q, fn, reads=(), writes=(), slot=None, cont=False):
        op = Op()
        op.q = q
        op.fn = fn
        op.slot = slot
        op.idx = len(self.ops)
        op.fill = None
        deps = {}
        lw, rd = self.last_writer, self.readers
        for r in reads:
            w = lw.get(r)
            if w is not None:
                deps[w.idx] = w
        for r in writes:
            w = lw.get(r)
            if w is not None:
                deps[w.idx] = w
            for x in rd.get(r, ()):
                deps[x.idx] = x
        for r in reads:
            rd.setdefault(r, []).append(op)
        for r in writes:
            lw[r] = op
            rd[r] = []
        if slot is not None:
            if slot not in self.slot_fill:
                self.slots.append(slot)
            prev = self.slot_fill.get(slot)
            if cont and prev is not None:
                op.fill = prev
            else:
                if prev is not None:
                    deps[prev.last.idx] = prev.last
                op.fill = Fill()
                self.slot_fill[slot] = op.fill
            op.fill.last = op
        deps.pop(op.idx, None)
        op.deps = list(deps.values())
        op.signals = False
        self.ops.append(op)
        return op

    def _src(self, op):
        return ("slot", op.slot) if op.slot is not None else ("q", op.q)

    def _skip(self, d, op):
        return (d.slot is None and op.slot is None and d.q == op.q and
                (d.q == "pe" or not self.same_engine_sync))

    def finalize(self):
        for op in self.ops:
            for d in op.deps:
                if self._skip(d, op):
                    continue
                d.signals = True
        counts = {}
        for op in self.ops:
            if op.slot is not None:
                op.signals = True
            if op.signals:
                s = self._src(op)
                inc = 16 if op.slot is not None else 1
                counts[s] = counts.get(s, 0) + inc
                op.sig = (s, counts[s], inc)
            else:
                op.sig = None
        self.final_counts = counts
        known = {q: {} for q in QUEUES}
        for op in self.ops:
            kq = known[op.q]
            need = {}
            for d in op.deps:
                if self._skip(d, op):
                    continue
                if d.slot is not None and d.fill is not op.fill:
                    d = d.fill.last
                    assert d.idx < op.idx, "consumer precedes end of DMA fill"
                s, c, _ = d.sig
                if kq.get(s, 0) >= c:
                    continue
                if need.get(s, (0, None))[0] < c:
                    need[s] = (c, d)
            waits = []
            for s, (c, d) in sorted(need.items(), key=lambda kv: -kv[1][1].idx):
                if kq.get(s, 0) >= c:
                    continue
                waits.append((s, c))
                for s2, c2 in d.known.items():
                    if kq.get(s2, 0) < c2:
                        kq[s2] = c2
                kq[s] = c
            op.waits = waits
            snap = dict(kq)
            if op.sig is not None:
                s, c, _ = op.sig
                if snap.get(s, 0) < c:
                    snap[s] = c
            op.known = snap

    def emit(self, nc, stack, final_wait_queue="sp"):
        self.finalize()
        sems = {}
        for q in QUEUES:
            sems[("q", q)] = stack.enter_context(nc.semaphore("sq_" + q))
        for i, sl in enumerate(self.slots):
            sems[("slot", sl)] = stack.enter_context(nc.semaphore("sd%d" % i))
        by_q = {q: [] for q in QUEUES}
        for op in self.ops:
            by_q[op.q].append(op)
        final = list(self.final_counts.items())

        def run(eng, q):
            for op in by_q[q]:
                for s, c in op.waits:
                    eng.wait_ge(sems[s], c)
                ins = op.fn(eng)
                if op.sig is not None:
                    s, c, inc = op.sig
                    ins.then_inc(sems[s], inc)
            if q == final_wait_queue:
                for s, c in final:
                    eng.wait_ge(sems[s], c)

        block = stack.enter_context(nc.Block())

        @block.tensor
        def _(e):
            run(e, "pe")

        @block.scalar
        def _(e):
            run(e, "act")

        @block.vector
        def _(e):
            run(e, "dve")

        @block.gpsimd
        def _(e):
            run(e, "pool")

        @block.sync
        def _(e):
            run(e, "sp")


class Grp:
    def __init__(self, key, c0, n, sample, g0, first):
        self.key, self.c0, self.n, self.sample, self.g0, self.first = key, c0, n, sample, g0, first
        self.ts = 16 if sample else 128
        self.nt = n // self.ts
        self.cols = slice(c0, c0 + n)


def build_program(depth=DEPTH, nsg=2, maxops=None):
    nc = bass.Bass("TRN2", target_bir_lowering=False)
    dram = lambda n, sh, kind: nc.dram_tensor(n, sh, F32, kind=kind).ap()
    xT_d = dram("xT", [D, NTOK], "ExternalInput")
    sgla_d = dram("sgla", [DEPTH, NS, 4, 64, 128], "ExternalInput")
    spool_d = dram("spool", [DEPTH, NS, 15, 512], "ExternalInput")
    win_d = dram("w_in", [DEPTH, D, IN_W], "ExternalInput")
    wout_d = dram("w_out", [DEPTH, D, D], "ExternalInput")
    wup_d = dram("w_up", [DEPTH, D, DFF], "ExternalInput")
    wdn_d = dram("w_down", [DEPTH, DFF, D], "ExternalInput")
    poolw_d = dram("pool_w", [DEPTH, 4, 128, 128], "ExternalInput")
    wgate_d = dram("w_gate", [DEPTH, 16, 256], "ExternalInput")
    bgate_d = dram("b_gate", [DEPTH, 256], "ExternalInput")
    pvec_d = dram("pvec", [128, NPV], "ExternalInput")
    con_d = dram("consts", [128, NCON], "ExternalInput")
    yT_d = dram("yT", [D, NTOK], "ExternalOutput")
    glap_d = dram("gla_p", [DEPTH, 128, 2, 128], "ExternalOutput")
    poolp_d = dram("pool_pT", [DEPTH, 128, 4, 15], "ExternalOutput")
    glas_d = dram("gla_s", [DEPTH, 128, NS, 2, 128], "ExternalOutput")
    pools_old_d = dram("pool_s_old", [DEPTH, NS, 14, 512], "ExternalOutput")
    pools_new_d = dram("pool_s_newT", [DEPTH, 128, 4, NS], "ExternalOutput")

    S = Sched()
    st = ExitStack()

    SB_LO, SB_HI = 16512, 229344
    cur = [SB_LO]

    def alloc(name, shape, dt, at=None):
        size = int(np.prod(shape[1:])) * (2 if dt == BF16 else 4)
        size = (size + 63) // 64 * 64
        if at is None:
            off = cur[0]
            cur[0] += size
        else:
            off = at[0]
            at[0] += size
        return nc.alloc_sbuf_tensor_at(name, list(shape), dt, offset=off)

    X = alloc("X", [128, 8, NT], F32)
    H = alloc("H", [128, 8, NT], BF16)
    Win = alloc("Win", [128, 8, IN_W], BF16)
    Wout = alloc("Wout", [128, 8, D], BF16)
    CON = alloc("CON", [128, NCON], F32)
    PV = alloc("PV", [128, NPV], F32)
    IDB = alloc("IDB", [128, 128], BF16)
    ONB = alloc("ONB", [128, 128], BF16)
    TRIB = alloc("TRIB", [128, 128], BF16)
    SST = alloc("SST", [128, DEPTH, 2, 128], F32)
    HALO = alloc("HALO", [128, DEPTH, 4, 15], F32)
    SBF = alloc("SBF", [128, 2, 2, 128], BF16)
    PWS = alloc("PWS", [128, 4, 128], BF16)
    PWN = alloc("PWN", [128, 4, 128], BF16)
    WSX = alloc("WSX", [128, 4096], BF16)
    WG = alloc("WG", [16, 256], BF16)
    BG = alloc("BG", [1, 256], BF16)
    PW = alloc("PW", [128, 4, 128], BF16)
    ov0 = cur[0]
    a = [ov0]
    AT = alloc("AT", [128, 16, NT], BF16, a)
    WS = [alloc("WS%d" % i, [128, 4096], BF16, a) for i in range(4)]
    RL = alloc("RL", [128, 2, 512], F32, a)
    mlp_end = a[0]
    a = [ov0]

    def mkset(i):
        d = {}
        d["i"] = i
        d["ALOW"] = alloc("ALOW%d" % i, [16, GM], BF16, a)
        d["VT"] = alloc("VT%d" % i, [128, 4, GM], BF16, a)
        d["SG"] = alloc("SG%d" % i, [128, 4, GM], F32, a)
        d["UT"] = alloc("UT%d" % i, [128, 4, GM + 15], F32, a)
        d["EQ"] = alloc("EQ%d" % i, [128, 2, GM], F32, a)
        d["EK"] = alloc("EK%d" % i, [128, 2, GM], F32, a)
        d["QT"] = alloc("QT%d" % i, [128, 2, GM], BF16, a)
        d["KT"] = alloc("KT%d" % i, [128, 2, GM], BF16, a)
        return d

    SETS = [mkset(0), mkset(1)]
    SS = {"i": "s"}
    SS["ALOW"] = alloc("ALOWs", [16, NS], BF16, a)
    SS["VT32"] = alloc("VT32s", [128, 4, NS], F32, a)
    SS["SG"] = alloc("SGs", [128, 4, NS], F32, a)
    SS["UT"] = alloc("UTs", [128, 4, NS + 15], F32, a)
    SS["K32"] = alloc("K32s", [128, 2, NS], F32, a)
    SS["EQ"] = alloc("EQs", [128, 2, NS], F32, a)
    SS["EK"] = alloc("EKs", [128, 2, NS], F32, a)
    SS["QT32"] = alloc("QT32s", [128, 2, NS], F32, a)
    SS["KT32"] = alloc("KT32s", [128, 2, NS], F32, a)
    PDs = alloc("PDs", [128, 4, NS], BF16, a)
    OTs = alloc("OTs", [128, 4, NS], F32, a)
    OIN = alloc("OIN", [128, 4, NS], F32, a)
    OSQs = alloc("OSQs", [128, 2, NS], BF16, a)
    RRs = alloc("RRs", [128, 2, NS], F32, a)
    MIXS = alloc("MIXS", [128, 8, NS], BF16, a)
    SQR3 = alloc("SQR3", [128, 2, NS], BF16, a)
    RT3 = alloc("RT3", [128, NS], F32, a)
    SQR1 = alloc("SQR1", [128, 2, GM], BF16, a)
    RT1 = alloc("RT1", [128, GM], F32, a)
    SQR2 = alloc("SQR2", [128, 2, 512], BF16, a)
    RT2 = alloc("RT2", [128, 512], F32, a)
    PT = alloc("PT", [128, 2, GM + 15], F32, a)
    PFX = alloc("PFX", [128, 16], F32, a)
    PD = alloc("PD", [128, 4, GM], BF16, a)
    LTOK = alloc("LTOK", [128, 2, 256], F32, a)
    LTOKB = alloc("LTOKB", [128, 2, 256], BF16, a)
    S0BF = alloc("S0BF", [128, 8, 2, 128], BF16, a)
    QTB = alloc("QTB", [128, 2, NS], BF16, a)
    KE = alloc("KE", [128, 2, GM], BF16, a)
    PROD = alloc("PROD", [128, 2, NS], F32, a)
    TMPS = alloc("TMPS", [128, 4, NS], F32, a)
    TOK = alloc("TOK", [128, 2, 768], BF16, a)
    ATTM = alloc("ATTM", [128, 2, 512], BF16, a)
    OT = alloc("OT", [128, 4, GM], F32, a)
    OSQ = alloc("OSQ", [128, 2, GM], BF16, a)
    RR = alloc("RR", [128, 2, GM], F32, a)
    MIXT = alloc("MIXT", [128, 8, 512], BF16, a)
    UB = alloc("UB", [128, 4, GM], BF16, a)
    S0_ = alloc("S0", [128, 8, 2, 128], F32, a)
    S0 = [S0_, S0_]
    XPOOL = alloc("XPOOL", [120, 512], F32, a)
    KMASK = alloc("KMASK", [16, 2, 256], BF16, a)
    TOKS = alloc("TOKS", [16, 768], BF16, a)
    mix_end = a[0]
    assert max(mlp_end, mix_end) <= SB_HI, (mlp_end, mix_end, SB_HI)

    NPS = 7
    P = [nc.alloc_psum_tensor("ps%d" % i, [128, 512], F32) for i in range(NPS)]
    PB = [nc.alloc_psum_tensor("pb%d" % i, [128, 1024], BF16) for i in range(1)]
    rr_ps = [0]
    rr_pb = [0]

    open_banks = set()

    def bank():
        for _ in range(NPS):
            i = rr_ps[0]
            rr_ps[0] = (i + 1) % NPS
            if ("ps", i) not in open_banks:
                return i
        raise RuntimeError("all PSUM banks are open")

    def bbank():
        for _ in range(1):
            i = rr_pb[0]
            rr_pb[0] = 0
            if ("pb", i) not in open_banks:
                return i
        raise RuntimeError("all bf16 PSUM banks are open")

    OVL = "OVL"

    def A(q, name, reads, writes, *args, **kw):
        for r in reads:
            open_banks.discard(r)
        return S.add(q, lambda e: getattr(e, name)(*args, **kw), reads=reads, writes=writes)

    def MM(out, lhsT, rhs, start, stop, reads, bk):
        open_banks.add(bk)
        return S.add("pe", lambda e: e.matmul(out, lhsT=lhsT, rhs=rhs, start=start, stop=stop),
                     reads=reads, writes=[bk])

    def TR(out, in_, ident, reads, bk):
        open_banks.add(bk)
        return S.add("pe", lambda e: e.transpose(out, in_, ident), reads=reads, writes=[bk])

    def DMA(q, out, in_, reads, writes, slot, cont=False):
        return S.add(q, lambda e: e.dma_start(out=out, in_=in_), reads=reads, writes=writes,
                     slot=slot, cont=cont)

    ident32 = CON[:, C_ID:C_ID + 128]
    triU = CON[:, C_TRI:C_TRI + 128]
    ones32 = CON[:, C_ONE:C_ONE + 128]
    invc = CON[:, C_INVC:C_INVC + 15]
    eps_ap = CON[:, C_EPS:C_EPS + 1]

    DMA("sp", CON[:], con_d, [], ["CON"], "con")
    DMA("sp", PV[:], pvec_d, [], ["PV"], "pv")
    A("dve", "tensor_copy", ["CON"], ["IDB"], out=IDB[:], in_=ident32)
    A("dve", "tensor_copy", ["CON"], ["ONB"], out=ONB[:], in_=ones32)
    A("dve", "tensor_copy", ["CON"], ["TRIB"], out=TRIB[:], in_=triU)

    def load_layer_weights(l):
        wv = win_d[l].rearrange("(k p) n -> p k n", p=128)
        DMA("pool", PW[:], poolw_d[l].rearrange("g c d -> c g d"), [], ["PW"], "pw")
        DMA("pool", WG[:], wgate_d[l], [], ["WG"], "wg")
        DMA("pool", BG[:], bgate_d[l:l + 1, :], [], ["BG"], "bg")
        DMA("pool", Win[:, :, 0:1032], wv[:, :, 0:1032], [], ["Win"], "win")
        DMA("pool", Win[:, :, 1032:IN_W], wv[:, :, 1032:IN_W], [], ["Win"], "win", cont=True)

    def load_wout(l):
        DMA("pool", Wout[:], wout_d[l].rearrange("(k p) n -> p k n", p=128), [], ["Wout"], "wout")

    def norm(cols, n, xkeys, gcol, out_fn, which):
        SQR, RT = {1: (SQR1, RT1), 2: (SQR2, RT2), 3: (SQR3, RT3)}[which]
        sk, rk = "SQR%d" % which, "RT%d" % which
        bk = bank()
        for c in range(9):
            if c < 8:
                sb = c % 2
                A("act", "activation", xkeys + [OVL], [(sk, sb)], out=SQR[:, sb, 0:n], in_=X[:, c, cols], func=AF.Square)
            if c >= 1:
                cm = c - 1
                MM(P[bk][:, 0:n], ONB[:], SQR[:, cm % 2, 0:n], cm == 0, cm == 7, [(sk, cm % 2), "ONB", OVL], ("ps", bk))
            if c % 2 == 1 or c == 8:
                yield
        A("act", "activation", [("ps", bk), "CON", OVL], [rk], out=RT[:, 0:n], in_=P[bk][:, 0:n], func=AF.Ln,
          scale=1.0 / D, bias=eps_ap)
        A("act", "activation", [rk, OVL], [rk], out=RT[:, 0:n], in_=RT[:, 0:n], func=AF.Exp, scale=-0.5)
        yield
        for c in range(8):
            o, ok, post = out_fn(c)
            A("dve", "scalar_tensor_tensor", xkeys + [rk, "PV", OVL], ok, out=o, in0=X[:, c, cols],
              scalar=PV[:, gcol + c:gcol + c + 1], in1=RT[:, 0:n], op0=ALU.mult, op1=ALU.mult)
            if post is not None:
                post()
            if c % 4 == 3:
                yield

    def part1(l, g, T):
        n, ts, nt = g.n, g.ts, g.nt
        si = T["i"]
        pv = l * PV_L
        hk = ("H", g.key)
        smp = g.sample
        K = lambda nm, *r: (nm, si) + r
        yield from norm(g.cols, g.n, [("X", g.key)], pv + PV_G1, lambda c: (H[:, c, g.cols], [hk], None), 1)

        pend = []

        def proj(col0, M, evac):
            bk = bank()
            for kc in range(8):
                MM(P[bk][0:M, 0:n], Win[:, kc, col0:col0 + M], H[:, kc, g.cols], kc == 0, kc == 7,
                   ["Win", hk], ("ps", bk))
            flush()
            pend.append(lambda: evac(bk))

        def flush():
            while pend:
                pend.pop(0)()

        def proj_v(h):
            if smp:
                proj(CV + h * 128, 128, lambda bk, h=h: A("act", "copy", [("ps", bk), OVL], [K("VT32")],
                                                          out=T["VT32"][:, h, 0:n], in_=P[bk][:, 0:n]))
            else:
                proj(CV + h * 128, 128, lambda bk, h=h: A("act", "copy", [("ps", bk), OVL], [K("VT")],
                                                          out=T["VT"][:, h, 0:n], in_=P[bk][:, 0:n]))

        proj(CA, 16, lambda bk: A("dve", "tensor_copy", [("ps", bk), OVL], [K("ALOW")], out=T["ALOW"][0:16, 0:n],
                                  in_=P[bk][0:16, 0:n]))
        proj_v(0)
        yield
        flush()
        yield
        bkx = bank()
        for t in range(nt):
            o = P[bkx][0:ts, t * 256:(t + 1) * 256]
            MM(o, T["ALOW"][0:16, t * ts:(t + 1) * ts], WG[0:16, :], True, False, [K("ALOW"), "WG", OVL], ("ps", bkx))
            MM(o, ONB[0:1, 0:ts], BG[0:1, :], False, True, ["ONB", "BG"], ("ps", bkx))
        ncol = nt * 256
        lt = LTOK[0:ts, :, :].rearrange("p a b -> p (a b)")[:, 0:ncol]
        A("act", "activation", [("ps", bkx), OVL], ["LTOK"], out=lt, in_=P[bkx][0:ts, 0:ncol], func=AF.Exp, scale=-1.0)
        ltb = LTOKB[0:ts, :, :].rearrange("p a b -> p (a b)")[:, 0:ncol]
        A("act", "activation", ["LTOK", OVL], ["LTOKB"], out=ltb, in_=lt, func=AF.Ln, bias=1.0)
        for h in range(1, 4):
            proj_v(h)
            yield
        cum = IDB if smp else TRIB
        for p in range(2):
            bkb = bank()
            for t in range(nt):
                MM(P[bkb][:, t * ts:(t + 1) * ts], LTOKB[0:ts, t, p * 128:(p + 1) * 128], cum[0:ts, 0:ts], True, True,
                   ["LTOKB", "TRIB", "IDB", OVL], ("ps", bkb))
            A("act", "activation", [("ps", bkb), OVL], [K("EQ", p)], out=T["EQ"][:, p, 0:n], in_=P[bkb][:, 0:n], func=AF.Exp,
              scale=-1.0 / 16)
            A("act", "activation", [("ps", bkb), OVL], [K("EK", p)], out=T["EK"][:, p, 0:n], in_=P[bkb][:, 0:n], func=AF.Exp,
              scale=1.0 / 16)
        yield
        if not smp:
            if g.first:
                A("pool", "memset", [OVL], [K("UT")], T["UT"][:, :, 0:15], 0.0)
            else:
                A("pool", "tensor_copy", ["HALO", OVL], [K("UT")], out=T["UT"][:, :, 0:15], in_=HALO[:, l, :, :])
        for gi in range(4):
            proj(CU + gi * 128, 128, lambda bk, gi=gi: A("act", "copy", [("ps", bk), OVL], [K("UT")],
                                                         out=T["UT"][:, gi, 15:15 + n], in_=P[bk][:, 0:n]))
            yield
        flush()
        if not smp:
            A("pool", "tensor_copy", [K("UT"), OVL], ["HALO"], out=HALO[:, l, :, :], in_=T["UT"][:, :, n:n + 15])
        qo, ko = (T["QT32"], T["KT32"]) if smp else (T["QT"], T["KT"])
        for p in range(2):
            proj(CQ + p * 128, 128, lambda bk, p=p: A("dve", "scalar_tensor_tensor", [("ps", bk), K("EQ", p), OVL], [K("QT")],
                                                      out=qo[:, p, 0:n], in0=P[bk][:, 0:n], scalar=0.125,
                                                      in1=T["EQ"][:, p, 0:n], op0=ALU.mult, op1=ALU.mult))
            yield
            if smp:
                flush()
                A("dve", "tensor_copy", [K("QT"), OVL], [K("QTB")], out=QTB[:, p, 0:n], in_=qo[:, p, 0:n])
                def ev(bk, p=p):
                    A("dve", "tensor_copy", [("ps", bk), OVL], [K("K32", p)], out=T["K32"][:, p, 0:n], in_=P[bk][:, 0:n])
                    A("dve", "tensor_tensor", [K("K32", p), K("EK", p), OVL], [K("KT")], out=ko[:, p, 0:n],
                      in0=T["K32"][:, p, 0:n], in1=T["EK"][:, p, 0:n], op=ALU.mult)
                proj(CK + p * 128, 128, ev)
            else:
                proj(CK + p * 128, 128, lambda bk, p=p: A("dve", "tensor_tensor", [("ps", bk), K("EK", p), OVL], [K("KT")],
                                                          out=ko[:, p, 0:n], in0=P[bk][:, 0:n], in1=T["EK"][:, p, 0:n],
                                                          op=ALU.mult))
            yield
        for h in range(4):
            proj(CG + h * 128, 128, lambda bk, h=h: A("act", "activation", [("ps", bk), OVL], [K("SG")],
                                                      out=T["SG"][:, h, 0:n], in_=P[bk][:, 0:n], func=AF.Silu))
            yield
        flush()
        yield

    def head_norm(l, g, T, mc, B):
        n = g.n
        si = T["i"]
        pv = l * PV_L
        tg = B["tag"]
        fine = B.get("fine", False)
        OT_, OSQ_, RR_, MX_ = B["OT"], B["OSQ"], B["RR"], B["MIXT"]
        K = lambda nm, *r: (nm, si) + r
        for h in range(4):
            sb = h % 2
            A("act", "activation", [("OT", tg), OVL], [("OSQ", tg, sb)], out=OSQ_[:, sb, 0:n], in_=OT_[:, h, 0:n], func=AF.Square)
            if fine:
                yield
            bk = bank()
            MM(P[bk][:, 0:n], ONB[:], OSQ_[:, sb, 0:n], True, True, [("OSQ", tg, sb), "ONB", OVL], ("ps", bk))
            if fine:
                yield
                yield
            A("act", "activation", [("ps", bk), "CON", OVL], [("RR", tg, sb)], out=RR_[:, sb, 0:n], in_=P[bk][:, 0:n], func=AF.Ln,
              scale=1.0 / 128, bias=eps_ap)
            A("act", "activation", [("RR", tg, sb), OVL], [("RR", tg, sb)], out=RR_[:, sb, 0:n], in_=RR_[:, sb, 0:n], func=AF.Exp,
              scale=-0.5)
            if fine:
                yield
            A("dve", "scalar_tensor_tensor", [("OT", tg), ("RR", tg, sb), "PV", OVL], [("RR", tg, sb)], out=RR_[:, sb, 0:n],
              in0=OT_[:, h, 0:n], scalar=PV[:, pv + PV_GG:pv + PV_GG + 1], in1=RR_[:, sb, 0:n], op0=ALU.mult, op1=ALU.mult)
            A("dve", "tensor_tensor", [("RR", tg, sb), K("SG"), OVL], [("MIXT", tg, mc)], out=MX_[:, h, mc:mc + n], in0=RR_[:, sb, 0:n],
              in1=T["SG"][:, h, 0:n], op=ALU.mult)
            yield

    def part2a(l, g, T, mc, B):
        n = g.n
        pv = l * PV_L
        smp = g.sample
        if smp:
            yield from gla_sample(l, g, T, B)
            yield from pool_sample(l, g, T, B)
        else:
            pool_prompt(l, g, T)
            yield
            yield from gla_prompt(l, g, T)
        assert smp or not mixt_busy[0], "MIXT would be overwritten before the previous pair's out-projection was emitted"
        yield from head_norm(l, g, T, mc, B)
        def pm_evac(gi, bk):
            A("act", "activation", [("ps", bk), "PV", OVL], [("MIXT", B["tag"], mc)], out=B["MIXT"][:, 4 + gi, mc:mc + n], in_=P[bk][:, 0:n],
              func=AF.Copy, scale=PV[:, pv + PV_PS + gi:pv + PV_PS + gi + 1])

        prev = None
        for gi in range(4):
            bk = bank()
            if smp:
                MM(P[bk][:, 0:n], PW[:, gi, :], B["PD"][:, gi, 0:n], True, True, ["PW", ("PD", B["tag"]), OVL], ("ps", bk))
                yield
                yield
                pm_evac(gi, bk)
            else:
                MM(P[bk][:, 0:n], PWS[:, gi, :], PD[:, gi, 0:n], True, False, ["PWS", ("PD", "p"), OVL], ("ps", bk))
                MM(P[bk][:, 0:n], PWN[:, gi, :], UB[:, gi, 0:n], False, True, ["PWS", "UB", OVL], ("ps", bk))
                if prev is not None:
                    pm_evac(*prev)
                prev = (gi, bk)
            yield
        if prev is not None:
            pm_evac(*prev)
            yield

    mixt_busy = [False]

    def part2b(l, keys, cols, n, mcs, B, which):
        pv = l * PV_L
        MX_ = B["MIXT"]
        if B["tag"] == "p":
            mixt_busy[0] = True
        xks = [("X", k) for k in keys]
        hks = [("H", k) for k in keys]
        mks = [("MIXT", B["tag"], m) for m in mcs]
        prev = None

        def add(dc, bk):
            A("dve", "tensor_tensor", [("ps", bk)] + xks, xks, out=X[:, dc, cols], in0=P[bk][:, 0:n], in1=X[:, dc, cols],
              op=ALU.add)

        for dc in range(8):
            bk = bank()
            for kc in range(8):
                MM(P[bk][:, 0:n], Wout[:, kc, dc * 128:(dc + 1) * 128], MX_[:, kc, 0:n], kc == 0, kc == 7,
                   ["Wout", OVL] + mks, ("ps", bk))
            if B.get("fine", False):
                yield
                yield
                add(dc, bk)
            else:
                if prev is not None:
                    add(*prev)
                prev = (dc, bk)
            if dc == 7 and B["tag"] == "p":
                mixt_busy[0] = False
            yield
        if prev is not None:
            add(*prev)
            yield
        yield from norm(cols, n, xks, pv + PV_G2, lambda c: (H[:, c, cols], hks, None), which)

    sbf_cur = [0]

    def gla_prompt(l, g, T):
        si = T["i"]
        K = lambda nm, *r: (nm, si) + r
        EQ, KT, QT, VT = T["EQ"], T["KT"], T["QT"], T["VT"]
        if g.first:
            A("dve", "memset", [], [("SST", l)], SST[:, l, :, :], 0.0)
        if g.key == 0:
            A("act", "copy", [("SST", l)], [("SBF", sbf_cur[0])], out=SBF[:, sbf_cur[0], :, :], in_=SST[:, l, :, :])
        for t in range(g.nt):
            tc_ = slice(t * 128, (t + 1) * 128)
            last = t * 128 + 127
            sc = sbf_cur[0]
            sn = 1 - sc
            sbf_cur[0] = sn
            tb = t % 2
            for p in range(2):
                A("dve", "tensor_scalar", [K("KT"), K("EQ", p), OVL], ["KE"], out=KE[:, p, tc_], in0=KT[:, p, tc_],
                  scalar1=EQ[:, p, last:last + 1], scalar2=None, op0=ALU.mult)
            bab = [bank(), bank()]
            for h in range(4):
                p, hh, r = h // 2, h % 2, slice((h % 2) * 64, (h % 2) * 64 + 64)
                MM(P[bab[hh]][:, p * 128:(p + 1) * 128], KT[r, p, tc_], QT[r, p, tc_], True, True, [K("KT"), K("QT"), OVL],
                   ("ps", bab[hh]))
            yield
            pb = bbank()
            for p in range(2):
                TR(PB[pb][:, p * 128:(p + 1) * 128], KE[:, p, tc_], IDB[:], ["KE", "IDB", OVL], ("pb", pb))
            for h in range(4):
                TR(PB[pb][:, 256 + h * 128:256 + (h + 1) * 128], VT[:, h, tc_], IDB[:], [K("VT"), "IDB", OVL], ("pb", pb))
            for hh in range(2):
                A("dve", "tensor_tensor", [("ps", bab[hh]), "CON", OVL], [("ATTM", tb)],
                  out=ATTM[:, tb, hh * 256:(hh + 1) * 256].rearrange("p (h c) -> p h c", h=2),
                  in0=P[bab[hh]][:, 0:256].rearrange("p (h c) -> p h c", h=2),
                  in1=triU.unsqueeze(1).to_broadcast([128, 2, 128]), op=ALU.mult)
            yield
            A("act", "copy", [("pb", pb), OVL], [("TOK", tb)], out=TOK[:, tb, :], in_=PB[pb][:, 0:768])
            yield
            bd = bank()
            for p in range(2):
                MM(P[bd][:, p * 256:(p + 1) * 256], TOK[:, tb, p * 128:(p + 1) * 128],
                   TOK[:, tb, 256 + 2 * p * 128:256 + (2 * p + 2) * 128], True, True, [("TOK", tb), OVL], ("ps", bd))
            bo = bank()
            for h in range(4):
                p, r = h // 2, slice((h % 2) * 64, (h % 2) * 64 + 64)
                o = P[bo][:, h * 128:(h + 1) * 128]
                ai = (h % 2) * 2 + h // 2
                MM(o, TOK[:, tb, 256 + h * 128:256 + (h + 1) * 128], ATTM[:, tb, ai * 128:(ai + 1) * 128], True, False,
                   [("TOK", tb), ("ATTM", tb), OVL], ("ps", bo))
                MM(o, SBF[r, sc, p, :], QT[r, p, tc_], False, True, [("SBF", sc), K("QT"), OVL], ("ps", bo))
            yield
            for p in range(2):
                for hh in range(2):
                    r = slice(hh * 64, hh * 64 + 64)
                    A("dve", "scalar_tensor_tensor", [("ps", bd), K("EQ", p), ("SST", l), OVL], [("SST", l)],
                      out=SST[r, l, p, :], in0=SST[r, l, p, :], scalar=EQ[r, p, last:last + 1],
                      in1=P[bd][r, p * 256 + hh * 128:p * 256 + (hh + 1) * 128], op0=ALU.mult, op1=ALU.add)
            A("act", "copy", [("ps", bo), OVL], [("OT", "p")], out=OT[:, :, tc_],
              in_=P[bo][:, :].rearrange("p (h c) -> p h c", h=4))
            yield
            A("act", "copy", [("SST", l), OVL], [("SBF", sn)], out=SBF[:, sn, :, :], in_=SST[:, l, :, :])
            yield

    def gla_sample(l, g, T, B):
        n = NS
        si = T["i"]
        K = lambda nm, *r: (nm, si) + r
        EQ, QT32, KT32, VT32, K32 = T["EQ"], T["QT32"], T["KT32"], T["VT32"], T["K32"]
        S0t = S0[0]

        def load(bh):
            for p in range(2):
                src = sgla_d[l, bh * 8:(bh + 1) * 8].rearrange("b h k v -> (h k) b v")[p * 128:(p + 1) * 128]
                DMA("sp", S0t[:, :, p, :], src, [OVL], [("S0", 0)], "s0_0", cont=(p == 1))

        load(0)
        yield
        bt = bank()
        for p in range(2):
            TR(P[bt][0:NS, p * 128:(p + 1) * 128], K32[:, p, 0:n], ident32, [K("K32", p), "CON", OVL], ("ps", bt))
        yield
        A("dve", "tensor_copy", [("ps", bt), OVL], ["TOKS"], out=TOKS[0:NS, 0:256], in_=P[bt][0:NS, 0:256])
        bt = bank()
        for h in range(4):
            TR(P[bt][0:NS, h * 128:(h + 1) * 128], VT32[:, h, 0:n], ident32, [K("VT32"), "CON", OVL], ("ps", bt))
        yield
        A("dve", "tensor_copy", [("ps", bt), OVL], ["TOKS"], out=TOKS[0:NS, 256:768], in_=P[bt][0:NS, 0:512])
        A("dve", "tensor_tensor", [K("QT"), K("KT"), OVL], ["PROD"], out=PROD[:, :, :], in0=QT32[:, :, :], in1=KT32[:, :, :],
          op=ALU.mult)
        yield
        bqb = [bank(), bank()]
        for h in range(4):
            p, hh, r = h // 2, h % 2, slice((h % 2) * 64, (h % 2) * 64 + 64)
            MM(P[bqb[hh]][:, p * NS:(p + 1) * NS], ones32[r, :], PROD[r, p, :], True, True, ["CON", "PROD", OVL],
               ("ps", bqb[hh]))
        yield
        for h in range(4):
            p, hh = h // 2, h % 2
            A("dve", "tensor_tensor", [("ps", bqb[hh]), K("VT32"), OVL], ["TMPS"], out=TMPS[:, h, :],
              in0=P[bqb[hh]][:, p * NS:(p + 1) * NS], in1=VT32[:, h, :], op=ALU.mult)
        yield
        for bh in range(2):
            if bh == 1:
                load(1)
                yield
                yield
                yield
            A("act", "copy", [("S0", 0), OVL], ["S0BF"], out=S0BF[:, :, :, :], in_=S0t[:, :, :, :])
            yield
            yield
            bob = [bank(), bank()]
            for bl in range(8):
                b = bh * 8 + bl
                for h in range(4):
                    p, hh, r = h // 2, h % 2, slice((h % 2) * 64, (h % 2) * 64 + 64)
                    MM(P[bob[hh]][:, p * NS + b:p * NS + b + 1], S0BF[r, bl, p, :], QTB[r, p, b:b + 1], True, True,
                       ["S0BF", K("QTB"), OVL], ("ps", bob[hh]))
            yield
            yield
            for h in range(4):
                p, hh = h // 2, h % 2
                A("act", "copy", [("ps", bob[hh]), OVL], ["OIN"], out=OIN[:, h, bh * 8:(bh + 1) * 8],
                  in_=P[bob[hh]][:, p * NS + bh * 8:p * NS + bh * 8 + 8])
            yield
            for bq in range(4):
                A("dve", "tensor_tensor", ["TOKS", "IDB", OVL], ["KMASK"], out=KMASK[0:NS, :, :],
                  in0=TOKS[0:NS, 0:256].unsqueeze(1).to_broadcast([NS, 2, 256]),
                  in1=IDB[0:NS, bh * 8 + bq * 2:bh * 8 + bq * 2 + 2].unsqueeze(2).to_broadcast([NS, 2, 256]), op=ALU.mult)
                yield
                bds = []
                for bl2 in range(2):
                    bd = bank()
                    bds.append(bd)
                    for p in range(2):
                        MM(P[bd][:, p * 256:(p + 1) * 256], KMASK[0:NS, bl2, p * 128:(p + 1) * 128],
                           TOKS[0:NS, 256 + 2 * p * 128:256 + (2 * p + 2) * 128], True, True, ["KMASK", "TOKS", OVL], ("ps", bd))
                yield
                yield
                for bl2 in range(2):
                    bl = bq * 2 + bl2
                    b = bh * 8 + bl
                    bd = bds[bl2]
                    for p in range(2):
                        for hh in range(2):
                            r = slice(hh * 64, hh * 64 + 64)
                            A("dve", "scalar_tensor_tensor", [("ps", bd), K("EQ", p), ("S0", 0), OVL], [("S0", 0)],
                              out=S0t[r, bl, p, :], in0=S0t[r, bl, p, :], scalar=EQ[r, p, b:b + 1],
                              in1=P[bd][r, p * 256 + hh * 128:p * 256 + (hh + 1) * 128], op0=ALU.mult, op1=ALU.add)
                    yield
            DMA("sp", glas_d[l][:, bh * 8:(bh + 1) * 8, :, :], S0t[:], [("S0", 0), OVL], [], "s0o_0")
            yield
        for h in range(4):
            A("dve", "tensor_tensor", ["OIN", "TMPS", OVL], [("OT", B["tag"])], out=B["OT"][:, h, 0:n],
              in0=OIN[:, h, :], in1=TMPS[:, h, :], op=ALU.add)
        yield

    def pool_prompt(l, g, T):
        n = g.n
        L = n + 15
        si = T["i"]
        UT = T["UT"]
        uk = ("UT", si)
        A("dve", "tensor_copy", [uk, OVL], ["UB"], out=UB[:, :, 0:n], in_=UT[:, :, 15:L])
        for gi, w in enumerate(POOL_W):
            src = UT[:, gi, :]
            steps = []
            sh = 1
            while sh < w:
                steps.append(sh)
                sh *= 2
            starts = [15]
            for s_ in reversed(steps):
                starts.append(starts[-1] - s_)
            starts = list(reversed(starts))
            cur_ap = src
            for i, s_ in enumerate(steps):
                a0 = starts[i + 1]
                lastlvl = (i == len(steps) - 1)
                if lastlvl and not g.first:
                    A("pool", "tensor_tensor", [uk, ("PT", 0), ("PT", 1), OVL], [("PD", "p")], out=PD[:, gi, 0:n],
                      in0=cur_ap[:, 15:L], in1=cur_ap[:, 15 - s_:L - s_], op=ALU.add)
                else:
                    dst = PT[:, i % 2, :]
                    A("pool", "tensor_tensor", [uk, ("PT", 0), ("PT", 1), OVL], [("PT", i % 2)], out=dst[:, a0:L],
                      in0=cur_ap[:, a0:L], in1=cur_ap[:, a0 - s_:L - s_], op=ALU.add)
                    cur_ap = dst
            if g.first:
                k = w - 1
                A("pool", "tensor_copy", [("PT", 0), ("PT", 1), OVL], [("PD", "p")], out=PD[:, gi, 0:n], in_=cur_ap[:, 15:L])
                A("pool", "tensor_tensor", [("PT", 0), ("PT", 1), "CON", OVL], [("PD", "p")], out=PD[:, gi, 0:k],
                  in0=cur_ap[:, 15:15 + k], in1=CON[:, C_WIC + gi * 15:C_WIC + gi * 15 + k], op=ALU.mult)

    def pool_sample(l, g, T, B):
        n = NS
        si = T["i"]
        UT = T["UT"]
        uk = ("UT", si)
        DMA("sp", pools_old_d[l], spool_d[l, :, 1:15, :], [], [], "psold")
        DMA("sp", pools_new_d[l], UT[:, :, 15:15 + n], [uk, OVL], [], "psnew")
        bk = None
        for bh in range(2):
            DMA("sp", XPOOL[:], spool_d[l, bh * 8:(bh + 1) * 8].rearrange("b r c -> (b r) c"), [OVL], ["XPOOL"], "xpool")
            yield
            yield
            if bk is None:
                bk = bank()
            for gi in range(4):
                MM(P[bk][:, gi * NS + bh * 8:gi * NS + bh * 8 + 8], XPOOL[:, gi * 128:(gi + 1) * 128],
                   CON[0:120, C_WSEL + gi * 8:C_WSEL + gi * 8 + 8], True, True, ["XPOOL", "CON", OVL], ("ps", bk))
            yield
        yield
        for gi, w in enumerate(POOL_W):
            A("dve", "scalar_tensor_tensor", [("ps", bk), uk, OVL], [("PD", B["tag"])], out=B["PD"][:, gi, 0:n], in0=UT[:, gi, 15:15 + n],
              scalar=1.0 / w - 1.0, in1=P[bk][:, gi * NS:(gi + 1) * NS], op0=ALU.mult, op1=ALU.add)
        yield

    def mlp(l, groups, nxt, prefetch_only=False):
        blocks = []
        for fh in range(2):
            for fb in range(4):
                blocks.append(("up", fh, fb))
            for db in range(4):
                blocks.append(("dn", fh, db))

        def bufof(i):
            return (WSX, ("WSX", 0), "wsx") if i == 0 else (WS[(i - 1) % 4], ("WS", (i - 1) % 4), "ws%d" % ((i - 1) % 4))

        def issue(i):
            kind, fh, j = blocks[i]
            wt, wkey, slot = bufof(i)
            if kind == "up":
                c0 = fh * 2048 + j * 512
                src = wup_d[l].rearrange("(k p) f -> p k f", p=128)[:, :, c0:c0 + 512]
                dst = wt[:, :].rearrange("p (k f) -> p k f", k=8)
            else:
                src = wdn_d[l, fh * 2048:(fh + 1) * 2048, j * 256:(j + 1) * 256].rearrange("(c p) d -> p c d", p=128)
                dst = wt[:, :].rearrange("p (c d) -> p c d", c=16)
            DMA("pool", dst, src, [] if i == 0 else [OVL], [wkey], slot)

        if prefetch_only:
            issue(0)
            return
        for i in range(1, 4):
            issue(i)
        rlb = [0]
        for i, (kind, fh, j) in enumerate(blocks):
            if i >= 1 and i + 3 < len(blocks):
                issue(i + 3)
            if i == 11 and nxt is not None:
                load_layer_weights(nxt)
            wt, wkey, _ = bufof(i)
            if kind == "up":
                wv = wt[:, :].rearrange("p (k f) -> p k f", k=8)
                for jj in range(4):
                    fc = j * 4 + jj
                    for keys, cols, n in groups:
                        bk = bank()
                        hks = [("H", k) for k in keys]
                        for kc in range(8):
                            MM(P[bk][:, 0:n], wv[:, kc, jj * 128:(jj + 1) * 128], H[:, kc, cols], kc == 0, kc == 7,
                               [wkey] + hks + [OVL], ("ps", bk))
                        rb = rlb[0]
                        rlb[0] = 1 - rb
                        A("act", "activation", [("ps", bk), OVL], [("RL", rb)], out=RL[:, rb, 0:n], in_=P[bk][:, 0:n],
                          func=AF.Relu)
                        A("dve", "tensor_tensor", [("RL", rb), OVL], [("AT", keys[0])], out=AT[:, fc, cols], in0=RL[:, rb, 0:n],
                          in1=RL[:, rb, 0:n], op=ALU.mult)
            else:
                wv = wt[:, :].rearrange("p (c d) -> p c d", c=16)
                for dl in range(2):
                    dc = j * 2 + dl
                    for keys, cols, n in groups:
                        bk = bank()
                        xks = [("X", k) for k in keys]
                        for fc in range(16):
                            MM(P[bk][:, 0:n], wv[:, fc, dl * 128:(dl + 1) * 128], AT[:, fc, cols], fc == 0, fc == 15,
                               [wkey, ("AT", keys[0]), OVL], ("ps", bk))
                        A("dve", "tensor_tensor", [("ps", bk)] + xks, xks, out=X[:, dc, cols], in0=P[bk][:, 0:n],
                          in1=X[:, dc, cols], op=ALU.add)

    def phase_switch():
        A("dve", "memset", [], [OVL], RL[:, 0, 0:8], 0.0)

    STAT = {'rounds': 0, 'bgsteps': 0, 'bgtail': 0}
    order = [(sg, l) for sg in range(nsg) for l in range(depth)]
    load_layer_weights(0)
    for sg in range(nsg):
        col0 = sg * 1024
        ncols = 1024 if sg == 0 else NT
        grps = [Grp(i, i * GM, GM, False, col0 + i * GM, sg == 0 and i == 0) for i in range(4)]
        if sg == 1:
            grps.append(Grp("s", 1024, NS, True, 2048, False))
        xkeys = [("X", g.key) for g in grps]
        xsrc = xT_d.rearrange("(c p) t -> p c t", p=128)
        DMA("sp", X[:, :, 0:512], xsrc[:, :, col0:col0 + 512], [], [("X", 0), ("X", 1)], "x0")
        DMA("sp", X[:, :, 512:1024], xsrc[:, :, col0 + 512:col0 + 1024], [], [("X", 2), ("X", 3)], "x1")
        if sg == 1:
            DMA("sp", X[:, :, 1024:NT], xsrc[:, :, 2048:NTOK], [], [("X", "s")], "x2")
        mgroups = [([0, 1], slice(0, 512), 512), ([2, 3], slice(512, 1024), 512)]
        if sg == 1:
            mgroups = [([0, 1], slice(0, 347), 347), ([1, 2], slice(347, 694), 347), ([2, 3, "s"], slice(694, 1040), 346)]
        for l in range(depth):
            idx = order.index((sg, l))
            nxt = order[idx + 1][1] if idx + 1 < len(order) else None
            for gi, w in enumerate(POOL_W):
                A("act", "activation", ["PW"], ["PWS"], out=PWS[:, gi, :], in_=PW[:, gi, :], func=AF.Copy, scale=1.0 / w)
            A("act", "activation", ["PW"], ["PWS"], out=PWN[:, :, :], in_=PW[:, :, :], func=AF.Copy, scale=-1.0)
            def rr_run(iters, bg=None):
                live = [x if isinstance(x, tuple) else (x, 1) for x in iters]
                while live:
                    for ent in list(live):
                        it, k = ent
                        for _ in range(k):
                            try:
                                next(it)
                            except StopIteration:
                                live.remove(ent)
                                break
                    STAT["rounds"] += 1
                    for _ in range(1):
                        if bg is not None and bg[0] is not None:
                            try:
                                next(bg[0])
                                STAT["bgsteps"] += 1
                            except StopIteration:
                                bg[0] = None

            PB2 = dict(OT=OT, OSQ=OSQ, RR=RR, PD=PD, MIXT=MIXT, tag="p")
            SB2 = dict(OT=OTs, OSQ=OSQs, RR=RRs, PD=PDs, MIXT=MIXS, tag="s", fine=True)
            pairs = [([0, 1], slice(0, 512), 512, [0, GM], PB2, 2), ([2, 3], slice(512, 1024), 512, [0, GM], PB2, 2)]
            pg = [g for g in grps if not g.sample]
            bg = [None]
            if sg == 1:
                gs = grps[-1]
                rr_run([part1(l, gs, SS)])

                def bg_gen(l=l, gs=gs):
                    yield from part2a(l, gs, SS, 0, SB2)
                    yield from part2b(l, ["s"], slice(1024, 1040), NS, [0], SB2, 3)

                bg[0] = bg_gen()
            load_wout(l)
            its1 = [part1(l, g, SETS[i % 2]) for i, g in enumerate(pg)]
            rr_run([its1[0]], bg)
            mlp(l, mgroups, None, prefetch_only=True)
            pending_b = None
            for i, g in enumerate(pg):
                mc = 0 if i % 2 == 0 else GM
                iters = [part2a(l, g, SETS[i % 2], mc, PB2)]
                if i + 1 < len(pg):
                    iters.append(its1[i + 1])
                if pending_b is not None:
                    iters.insert(0, (part2b(l, *pending_b), 2))
                    pending_b = None
                rr_run(iters, bg)
                if i % 2 == 1:
                    pending_b = pairs[i // 2]
            rr_run([part2b(l, *pending_b)], bg)
            while bg[0] is not None:
                rr_run([], bg) if False else None
                try:
                    next(bg[0])
                    STAT["bgtail"] += 1
                except StopIteration:
                    bg[0] = None
            if sg == 1:
                DMA("sp", glap_d[l], SST[:, l, :, :], [("SST", l)], [], "glap")
                DMA("sp", poolp_d[l], HALO[:, l, :, :], ["HALO"], [], "poolp")
            phase_switch()
            mlp(l, mgroups, nxt)
            phase_switch()
        OTS = OT[:, :, :].rearrange("p (a h) c -> p a (h c)", a=2)
        fpairs = [([0, 1], slice(0, 512), 512, col0), ([2, 3], slice(512, 1024), 512, col0 + 512)]
        if sg == 1:
            fpairs.append((["s"], slice(1024, 1040), NS, 2048))
        for keys, cols, n, gcol0 in fpairs:
            def out_fn(c, n=n, gcol0=gcol0):
                yb_ = c % 2
                def post():
                    DMA("sp", yT_d[c * 128:(c + 1) * 128, gcol0:gcol0 + n], OTS[:, yb_, 0:n], [("YS", yb_), OVL], [],
                        "yo%d" % yb_)
                return OTS[:, yb_, 0:n], [("YS", yb_), ("OT", "p")], post
            for _ in norm(cols, n, [("X", k) for k in keys], PV_FIN, out_fn, 2):
                pass
        phase_switch()

    print('n_ops', len(S.ops), STAT)
    if maxops is not None:
        S.ops = S.ops[:maxops]
    S.emit(nc, st)
    st.close()
    return nc


_NC_CACHE = {}


def _host_consts():
    con = np.zeros((128, NCON), np.float32)
    con[:, C_ID:C_ID + 128] = np.eye(128, dtype=np.float32)
    con[:, C_TRI:C_TRI + 128] = np.triu(np.ones((128, 128), np.float32))
    con[:, C_ONE:C_ONE + 128] = 1.0
    con[:, C_INVC:C_INVC + 15] = (1.0 / np.arange(1, 16, dtype=np.float64)).astype(np.float32)[None, :]
    con[:, C_EPS] = EPS
    for gi, w in enumerate(POOL_W):
        for b in range(8):
            for r in range(15 - (w - 1), 15):
                con[b * 15 + r, C_WSEL + gi * 8 + b] = 1.0 / w
    for gi, w in enumerate(POOL_W):
        con[:, C_WIC + gi * 15:C_WIC + gi * 15 + 15] = (w / np.arange(1, 16, dtype=np.float64)).astype(np.float32)[None, :]
    return con


def kernel(x_prompt, x_sample, state_gla, state_pool, norm1_g, w_in, w_gate, b_gate, gla_norm_g, pool_w,
           pool_scale, w_out, norm2_g, w_up, w_down, final_g):
    f = lambda a: np.ascontiguousarray(np.asarray(a, dtype=np.float32))
    x_prompt, x_sample, state_gla, state_pool = f(x_prompt), f(x_sample), f(state_gla), f(state_pool)
    w_in, w_out, w_up, w_down = f(w_in), f(w_out), f(w_up), f(w_down)
    pool_w, w_gate, b_gate = f(pool_w), f(w_gate), f(b_gate)
    pvec = np.zeros((128, NPV), np.float32)
    n1, n2, gg, ps, fg = f(norm1_g), f(norm2_g), f(gla_norm_g), f(pool_scale), f(final_g)
    for l in range(DEPTH):
        b = l * PV_L
        pvec[:, b + PV_G1:b + PV_G1 + 8] = n1[l].reshape(8, 128).T
        pvec[:, b + PV_G2:b + PV_G2 + 8] = n2[l].reshape(8, 128).T
        pvec[:, b + PV_GG] = gg[l]
        pvec[:, b + PV_PS:b + PV_PS + 4] = ps[l].reshape(4, 128).T
    pvec[:, PV_FIN:PV_FIN + 8] = fg.reshape(8, 128).T
    con = _host_consts()
    if "nc" not in _NC_CACHE:
        _NC_CACHE["nc"] = build_program()
    nc = _NC_CACHE["nc"]
    in_maps = []
    for c in range(NCORE):
        xT = np.empty((D, NTOK), np.float32)
        xT[:, :SEQ] = x_prompt[c].T
        xT[:, SEQ:] = x_sample[c * NS:(c + 1) * NS, 0, :].T
        in_maps.append({
            "xT": xT,
            "sgla": np.ascontiguousarray(state_gla[:, c * NS:(c + 1) * NS]),
            "spool": np.ascontiguousarray(state_pool[:, c * NS:(c + 1) * NS]),
            "w_in": w_in, "w_out": w_out, "w_up": w_up, "w_down": w_down,
            "pool_w": pool_w, "w_gate": w_gate, "b_gate": b_gate, "pvec": pvec, "consts": con,
        })
    res = run_bass_kernel_spmd(nc, in_maps, core_ids=list(range(NCORE)))
    B = NCORE
    y_prompt = np.empty((B, SEQ, D), np.float32)
    y_sample = np.empty((B * NS, 1, D), np.float32)
    gla_p = np.empty((DEPTH, B, 4, 64, 128), np.float32)
    pool_p = np.empty((DEPTH, B, 15, 512), np.float32)
    gla_s = np.empty((DEPTH, B * NS, 4, 64, 128), np.float32)
    pool_s = np.empty((DEPTH, B * NS, 15, 512), np.float32)
    for c in range(NCORE):
        r = res.results[c]
        yT = np.asarray(r["yT"])
        y_prompt[c] = yT[:, :SEQ].T
        y_sample[c * NS:(c + 1) * NS, 0, :] = yT[:, SEQ:].T
        gp = np.asarray(r["gla_p"]).reshape(DEPTH, 2, 64, 2, 128)
        gla_p[:, c] = gp.transpose(0, 3, 1, 2, 4).reshape(DEPTH, 4, 64, 128)
        pp = np.asarray(r["pool_pT"])
        pool_p[:, c] = pp.transpose(0, 3, 2, 1).reshape(DEPTH, 15, 512)
        gs = np.asarray(r["gla_s"]).reshape(DEPTH, 2, 64, NS, 2, 128)
        gla_s[:, c * NS:(c + 1) * NS] = gs.transpose(0, 3, 4, 1, 2, 5).reshape(DEPTH, NS, 4, 64, 128)
        pool_s[:, c * NS:(c + 1) * NS, 0:14] = np.asarray(r["pool_s_old"])
        pn = np.asarray(r["pool_s_newT"])
        pool_s[:, c * NS:(c + 1) * NS, 14] = pn.transpose(0, 3, 2, 1).reshape(DEPTH, NS, 512)
    return (y_prompt, y_sample, gla_p, pool_p, gla_s, pool_s)
```

```python
import numpy as np
from contextlib import ExitStack
import concourse.bass as bass
import concourse.mybir as mybir
from concourse.bass_utils import run_bass_kernel_spmd

F32 = mybir.dt.float32
BF16 = mybir.dt.bfloat16
ALU = mybir.AluOpType
AF = mybir.ActivationFunctionType

QUEUES = ("pe", "act", "dve", "pool", "sp")

D = 1024
DEPTH = 4
NCORE = 8
SEQ = 2048
NS = 16
NTOK = SEQ + NS
IN_W = 2064
DFF = 4096
NT = 1040
GM = 256
EPS = 1e-6
POOL_W = (2, 4, 8, 16)
CQ, CK, CV, CG, CA, CU = 0, 256, 512, 1024, 1536, 1552
C_ID, C_TRI, C_ONE, C_INVC, C_EPS, C_WSEL, C_WIC, NCON = 0, 128, 256, 384, 399, 400, 432, 492
PV_L, PV_G1, PV_G2, PV_GG, PV_PS, PV_FIN, NPV = 21, 0, 8, 16, 17, 84, 92


class Op:
    __slots__ = ("q", "fn", "deps", "slot", "fill", "sig", "waits", "known", "signals", "idx")


class Fill:
    __slots__ = ("last",)


class Sched:
    def __init__(self, same_engine_sync=True):
        self.ops = []
        self.last_writer = {}
        self.readers = {}
        self.same_engine_sync = same_engine_sync
        self.slots = []
        self.slot_fill = {}

    def add(self, q, fn, reads=(), writes=(), slot=None, cont=False):
        op = Op()
        op.q = q
        op.fn = fn
        op.slot = slot
        op.idx = len(self.ops)
        op.fill = None
        deps = {}
        lw, rd = self.last_writer, self.readers
        for r in reads:
            w = lw.get(r)
            if w is not None:
                deps[w.idx] = w
        for r in writes:
            w = lw.get(r)
            if w is not None:
                deps[w.idx] = w
            for x in rd.get(r, ()):
                deps[x.idx] = x
        for r in reads:
            rd.setdefault(r, []).append(op)
        for r in writes:
            lw[r] = op
            rd[r] = []
        if slot is not None:
            if slot not in self.slot_fill:
                self.slots.append(slot)
            prev = self.slot_fill.get(slot)
            if cont and prev is not None:
                op.fill = prev
            else:
                if prev is not None:
                    deps[prev.last.idx] = prev.last
                op.fill = Fill()
                self.slot_fill[slot] = op.fill
            op.fill.last = op
        deps.pop(op.idx, None)
        op.deps = list(deps.values())
        op.signals = False
        self.ops.append(op)
        return op

    def _src(self, op):
        return ("slot", op.slot) if op.slot is not None else ("q", op.q)

    def _skip(self, d, op):
        return (d.slot is None and op.slot is None and d.q == op.q and
                (d.q == "pe" or not self.same_engine_sync))

    def finalize(self):
        for op in self.ops:
            for d in op.deps:
                if self._skip(d, op):
                    continue
                d.signals = True
        counts = {}
        for op in self.ops:
            if op.slot is not None:
                op.signals = True
            if op.signals:
                s = self._src(op)
                inc = 16 if op.slot is not None else 1
                counts[s] = counts.get(s, 0) + inc
                op.sig = (s, counts[s], inc)
            else:
                op.sig = None
        self.final_counts = counts
        known = {q: {} for q in QUEUES}
        for op in self.ops:
            kq = known[op.q]
            need = {}
            for d in op.deps:
                if self._skip(d, op):
                    continue
                if d.slot is not None and d.fill is not op.fill:
                    d = d.fill.last
                    assert d.idx < op.idx, "consumer precedes end of DMA fill"
                s, c, _ = d.sig
                if kq.get(s, 0) >= c:
                    continue
                if need.get(s, (0, None))[0] < c:
                    need[s] = (c, d)
            waits = []
            for s, (c, d) in sorted(need.items(), key=lambda kv: -kv[1][1].idx):
                if kq.get(s, 0) >= c:
                    continue
                waits.append((s, c))
                for s2, c2 in d.known.items():
                    if kq.get(s2, 0) < c2:
                        kq[s2] = c2
                kq[s] = c
            op.waits = waits
            snap = dict(kq)
            if op.sig is not None:
                s, c, _ = op.sig
                if snap.get(s, 0) < c:
                    snap[s] = c
            op.known = snap

    def emit(self, nc, stack, final_wait_queue="sp"):
        self.finalize()
        sems = {}
        for q in QUEUES:
            sems[("q", q)] = stack.enter_context(nc.semaphore("sq_" + q))
        for i, sl in enumerate(self.slots):
            sems[("slot", sl)] = stack.enter_context(nc.semaphore("sd%d" % i))
        by_q = {q: [] for q in QUEUES}
        for op in self.ops:
            by_q[op.q].append(op)
        final = list(self.final_counts.items())

        def run(eng, q):
            for op in by_q[q]:
                for s, c in op.waits:
                    eng.wait_ge(sems[s], c)
                ins = op.fn(eng)
                if op.sig is not None:
                    s, c, inc = op.sig
                    ins.then_inc(sems[s], inc)
            if q == final_wait_queue:
                for s, c in final:
                    eng.wait_ge(sems[s], c)

        block = stack.enter_context(nc.Block())

        @block.tensor
        def _(e):
            run(e, "pe")

        @block.scalar
        def _(e):
            run(e, "act")

        @block.vector
        def _(e):
            run(e, "dve")

        @block.gpsimd
        def _(e):
            run(e, "pool")

        @block.sync
        def _(e):
            run(e, "sp")


class Grp:
    def __init__(self, key, c0, n, sample, g0, first):
        self.key, self.c0, self.n, self.sample, self.g0, self.first = key, c0, n, sample, g0, first
        self.ts = 16 if sample else 128
        self.nt = n // self.ts
        self.cols = slice(c0, c0 + n)


def build_program(depth=DEPTH, nsg=2, maxops=None):
    nc = bass.Bass("TRN2", target_bir_lowering=False)
    dram = lambda n, sh, kind: nc.dram_tensor(n, sh, F32, kind=kind).ap()
    xT_d = dram("xT", [D, NTOK], "ExternalInput")
    sgla_d = dram("sgla", [DEPTH, NS, 4, 64, 128], "ExternalInput")
    spool_d = dram("spool", [DEPTH, NS, 15, 512], "ExternalInput")
    win_d = dram("w_in", [DEPTH, D, IN_W], "ExternalInput")
    wout_d = dram("w_out", [DEPTH, D, D], "ExternalInput")
    wup_d = dram("w_up", [DEPTH, D, DFF], "ExternalInput")
    wdn_d = dram("w_down", [DEPTH, DFF, D], "ExternalInput")
    poolw_d = dram("pool_w", [DEPTH, 4, 128, 128], "ExternalInput")
    wgate_d = dram("w_gate", [DEPTH, 16, 256], "ExternalInput")
    bgate_d = dram("b_gate", [DEPTH, 256], "ExternalInput")
    pvec_d = dram("pvec", [128, NPV], "ExternalInput")
    con_d = dram("consts", [128, NCON], "ExternalInput")
    yT_d = dram("yT", [D, NTOK], "ExternalOutput")
    glap_d = dram("gla_p", [DEPTH, 128, 2, 128], "ExternalOutput")
    poolp_d = dram("pool_pT", [DEPTH, 128, 4, 15], "ExternalOutput")
    glas_d = dram("gla_s", [DEPTH, 128, NS, 2, 128], "ExternalOutput")
    pools_old_d = dram("pool_s_old", [DEPTH, NS, 14, 512], "ExternalOutput")
    pools_new_d = dram("pool_s_newT", [DEPTH, 128, 4, NS], "ExternalOutput")

    S = Sched()
    st = ExitStack()

    SB_LO, SB_HI = 16512, 229344
    cur = [SB_LO]

    def alloc(name, shape, dt, at=None):
        size = int(np.prod(shape[1:])) * (2 if dt == BF16 else 4)
        size = (size + 63) // 64 * 64
        if at is None:
            off = cur[0]
            cur[0] += size
        else:
            off = at[0]
            at[0] += size
        return nc.alloc_sbuf_tensor_at(name, list(shape), dt, offset=off)

    X = alloc("X", [128, 8, NT], F32)
    H = alloc("H", [128, 8, NT], BF16)
    Win = alloc("Win", [128, 8, IN_W], BF16)
    Wout = alloc("Wout", [128, 8, D], BF16)
    CON = alloc("CON", [128, NCON], F32)
    PV = alloc("PV", [128, NPV], F32)
    IDB = alloc("IDB", [128, 128], BF16)
    ONB = alloc("ONB", [128, 128], BF16)
    TRIB = alloc("TRIB", [128, 128], BF16)
    SST = alloc("SST", [128, DEPTH, 2, 128], F32)
    HALO = alloc("HALO", [128, DEPTH, 4, 15], F32)
    SBF = alloc("SBF", [128, 2, 2, 128], BF16)
    PWS = alloc("PWS", [128, 4, 128], BF16)
    PWN = alloc("PWN", [128, 4, 128], BF16)
    WSX = alloc("WSX", [128, 4096], BF16)
    WG = alloc("WG", [16, 256], BF16)
    BG = alloc("BG", [1, 256], BF16)
    PW = alloc("PW", [128, 4, 128], BF16)
    ov0 = cur[0]
    a = [ov0]
    AT = alloc("AT", [128, 16, NT], BF16, a)
    WS = [alloc("WS%d" % i, [128, 4096], BF16, a) for i in range(4)]
    RL = alloc("RL", [128, 2, 512], F32, a)
    mlp_end = a[0]
    a = [ov0]

    def mkset(i):
        d = {}
        d["i"] = i
        d["ALOW"] = alloc("ALOW%d" % i, [16, GM], BF16, a)
        d["VT"] = alloc("VT%d" % i, [128, 4, GM], BF16, a)
        d["SG"] = alloc("SG%d" % i, [128, 4, GM], F32, a)
        d["UT"] = alloc("UT%d" % i, [128, 4, GM + 15], F32, a)
        d["EQ"] = alloc("EQ%d" % i, [128, 2, GM], F32, a)
        d["EK"] = alloc("EK%d" % i, [128, 2, GM], F32, a)
        d["QT"] = alloc("QT%d" % i, [128, 2, GM], BF16, a)
        d["KT"] = alloc("KT%d" % i, [128, 2, GM], BF16, a)
        return d

    SETS = [mkset(0), mkset(1)]
    SS = {"i": "s"}
    SS["ALOW"] = alloc("ALOWs", [16, NS], BF16, a)
    SS["VT32"] = alloc("VT32s", [128, 4, NS], F32, a)
    SS["SG"] = alloc("SGs", [128, 4, NS], F32, a)
    SS["UT"] = alloc("UTs", [128, 4, NS + 15], F32, a)
    SS["K32"] = alloc("K32s", [128, 2, NS], F32, a)
    SS["EQ"] = alloc("EQs", [128, 2, NS], F32, a)
    SS["EK"] = alloc("EKs", [128, 2, NS], F32, a)
    SS["QT32"] = alloc("QT32s", [128, 2, NS], F32, a)
    SS["KT32"] = alloc("KT32s", [128, 2, NS], F32, a)
    PDs = alloc("PDs", [128, 4, NS], BF16, a)
    OTs = alloc("OTs", [128, 4, NS], F32, a)
    OIN = alloc("OIN", [128, 4, NS], F32, a)
    OSQs = alloc("OSQs", [128, 2, NS], BF16, a)
    RRs = alloc("RRs", [128, 2, NS], F32, a)
    MIXS = alloc("MIXS", [128, 8, NS], BF16, a)
    SQR3 = alloc("SQR3", [128, 2, NS], BF16, a)
    RT3 = alloc("RT3", [128, NS], F32, a)
    SQR1 = alloc("SQR1", [128, 2, GM], BF16, a)
    RT1 = alloc("RT1", [128, GM], F32, a)
    SQR2 = alloc("SQR2", [128, 2, 512], BF16, a)
    RT2 = alloc("RT2", [128, 512], F32, a)
    PT = alloc("PT", [128, 2, GM + 15], F32, a)
    PFX = alloc("PFX", [128, 16], F32, a)
    PD = alloc("PD", [128, 4, GM], BF16, a)
    LTOK = alloc("LTOK", [128, 2, 256], F32, a)
    LTOKB = alloc("LTOKB", [128, 2, 256], BF16, a)
    S0BF = alloc("S0BF", [128, 8, 2, 128], BF16, a)
    QTB = alloc("QTB", [128, 2, NS], BF16, a)
    KE = alloc("KE", [128, 2, GM], BF16, a)
    PROD = alloc("PROD", [128, 2, NS], F32, a)
    TMPS = alloc("TMPS", [128, 4, NS], F32, a)
    TOK = alloc("TOK", [128, 2, 768], BF16, a)
    ATTM = alloc("ATTM", [128, 2, 512], BF16, a)
    OT = alloc("OT", [128, 4, GM], F32, a)
    OSQ = alloc("OSQ", [128, 2, GM], BF16, a)
    RR = alloc("RR", [128, 2, GM], F32, a)
    MIXT = alloc("MIXT", [128, 8, 512], BF16, a)
    UB = alloc("UB", [128, 4, GM], BF16, a)
    S0_ = alloc("S0", [128, 8, 2, 128], F32, a)
    S0 = [S0_, S0_]
    XPOOL = alloc("XPOOL", [120, 512], F32, a)
    KMASK = alloc("KMASK", [16, 2, 256], BF16, a)
    TOKS = alloc("TOKS", [16, 768], BF16, a)
    mix_end = a[0]
    assert max(mlp_end, mix_end) <= SB_HI, (mlp_end, mix_end, SB_HI)

    NPS = 7
    P = [nc.alloc_psum_tensor("ps%d" % i, [128, 512], F32) for i in range(NPS)]
    PB = [nc.alloc_psum_tensor("pb%d" % i, [128, 1024], BF16) for i in range(1)]
    rr_ps = [0]
    rr_pb = [0]

    open_banks = set()

    def bank():
        for _ in range(NPS):
            i = rr_ps[0]
            rr_ps[0] = (i + 1) % NPS
            if ("ps", i) not in open_banks:
                return i
        raise RuntimeError("all PSUM banks are open")

    def bbank():
        for _ in range(1):
            i = rr_pb[0]
            rr_pb[0] = 0
            if ("pb", i) not in open_banks:
                return i
        raise RuntimeError("all bf16 PSUM banks are open")

    OVL = "OVL"

    def A(q, name, reads, writes, *args, **kw):
        for r in reads:
            open_banks.discard(r)
        return S.add(q, lambda e: getattr(e, name)(*args, **kw), reads=reads, writes=writes)

    def MM(out, lhsT, rhs, start, stop, reads, bk):
        open_banks.add(bk)
        return S.add("pe", lambda e: e.matmul(out, lhsT=lhsT, rhs=rhs, start=start, stop=stop),
                     reads=reads, writes=[bk])

    def TR(out, in_, ident, reads, bk):
        open_banks.add(bk)
        return S.add("pe", lambda e: e.transpose(out, in_, ident), reads=reads, writes=[bk])

    def DMA(q, out, in_, reads, writes, slot, cont=False):
        return S.add(q, lambda e: e.dma_start(out=out, in_=in_), reads=reads, writes=writes,
                     slot=slot, cont=cont)

    ident32 = CON[:, C_ID:C_ID + 128]
    triU = CON[:, C_TRI:C_TRI + 128]
    ones32 = CON[:, C_ONE:C_ONE + 128]
    invc = CON[:, C_INVC:C_INVC + 15]
    eps_ap = CON[:, C_EPS:C_EPS + 1]

    DMA("sp", CON[:], con_d, [], ["CON"], "con")
    DMA("sp", PV[:], pvec_d, [], ["PV"], "pv")
    A("dve", "tensor_copy", ["CON"], ["IDB"], out=IDB[:], in_=ident32)
    A("dve", "tensor_copy", ["CON"], ["ONB"], out=ONB[:], in_=ones32)
    A("dve", "tensor_copy", ["CON"], ["TRIB"], out=TRIB[:], in_=triU)

    def load_layer_weights(l):
        wv = win_d[l].rearrange("(k p) n -> p k n", p=128)
        DMA("pool", PW[:], poolw_d[l].rearrange("g c d -> c g d"), [], ["PW"], "pw")
        DMA("pool", WG[:], wgate_d[l], [], ["WG"], "wg")
        DMA("pool", BG[:], bgate_d[l:l + 1, :], [], ["BG"], "bg")
        DMA("pool", Win[:, :, 0:1032], wv[:, :, 0:1032], [], ["Win"], "win")
        DMA("pool", Win[:, :, 1032:IN_W], wv[:, :, 1032:IN_W], [], ["Win"], "win", cont=True)

    def load_wout(l):
        DMA("pool", Wout[:], wout_d[l].rearrange("(k p) n -> p k n", p=128), [], ["Wout"], "wout")

    def norm(cols, n, xkeys, gcol, out_fn, which):
        SQR, RT = {1: (SQR1, RT1), 2: (SQR2, RT2), 3: (SQR3, RT3)}[which]
        sk, rk = "SQR%d" % which, "RT%d" % which
        bk = bank()
        for c in range(8):
            sb = c % 2
            A("act", "activation", xkeys + [OVL], [(sk, sb)], out=SQR[:, sb, 0:n], in_=X[:, c, cols], func=AF.Square)
            MM(P[bk][:, 0:n], ONB[:], SQR[:, sb, 0:n], c == 0, c == 7, [(sk, sb), "ONB", OVL], ("ps", bk))
            if c % 4 == 3:
                yield
        A("act", "activation", [("ps", bk), "CON", OVL], [rk], out=RT[:, 0:n], in_=P[bk][:, 0:n], func=AF.Ln,
          scale=1.0 / D, bias=eps_ap)
        A("act", "activation", [rk, OVL], [rk], out=RT[:, 0:n], in_=RT[:, 0:n], func=AF.Exp, scale=-0.5)
        yield
        for c in range(8):
            o, ok, post = out_fn(c)
            A("dve", "scalar_tensor_tensor", xkeys + [rk, "PV", OVL], ok, out=o, in0=X[:, c, cols],
              scalar=PV[:, gcol + c:gcol + c + 1], in1=RT[:, 0:n], op0=ALU.mult, op1=ALU.mult)
            if post is not None:
                post()
            if c % 4 == 3:
                yield

    def part1(l, g, T):
        n, ts, nt = g.n, g.ts, g.nt
        si = T["i"]
        pv = l * PV_L
        hk = ("H", g.key)
        smp = g.sample
        K = lambda nm, *r: (nm, si) + r
        yield from norm(g.cols, g.n, [("X", g.key)], pv + PV_G1, lambda c: (H[:, c, g.cols], [hk], None), 1)

        pend = []

        def proj(col0, M, evac):
            bk = bank()
            for kc in range(8):
                MM(P[bk][0:M, 0:n], Win[:, kc, col0:col0 + M], H[:, kc, g.cols], kc == 0, kc == 7,
                   ["Win", hk], ("ps", bk))
            flush()
            pend.append(lambda: evac(bk))

        def flush():
            while pend:
                pend.pop(0)()

        def proj_v(h):
            if smp:
                proj(CV + h * 128, 128, lambda bk, h=h: A("act", "copy", [("ps", bk), OVL], [K("VT32")],
                                                          out=T["VT32"][:, h, 0:n], in_=P[bk][:, 0:n]))
            else:
                proj(CV + h * 128, 128, lambda bk, h=h: A("act", "copy", [("ps", bk), OVL], [K("VT")],
                                                          out=T["VT"][:, h, 0:n], in_=P[bk][:, 0:n]))

        proj(CA, 16, lambda bk: A("dve", "tensor_copy", [("ps", bk), OVL], [K("ALOW")], out=T["ALOW"][0:16, 0:n],
                                  in_=P[bk][0:16, 0:n]))
        proj_v(0)
        yield
        flush()
        yield
        bkx = bank()
        for t in range(nt):
            o = P[bkx][0:ts, t * 256:(t + 1) * 256]
            MM(o, T["ALOW"][0:16, t * ts:(t + 1) * ts], WG[0:16, :], True, False, [K("ALOW"), "WG", OVL], ("ps", bkx))
            MM(o, ONB[0:1, 0:ts], BG[0:1, :], False, True, ["ONB", "BG"], ("ps", bkx))
        ncol = nt * 256
        lt = LTOK[0:ts, :, :].rearrange("p a b -> p (a b)")[:, 0:ncol]
        A("act", "activation", [("ps", bkx), OVL], ["LTOK"], out=lt, in_=P[bkx][0:ts, 0:ncol], func=AF.Exp, scale=-1.0)
        ltb = LTOKB[0:ts, :, :].rearrange("p a b -> p (a b)")[:, 0:ncol]
        A("act", "activation", ["LTOK", OVL], ["LTOKB"], out=ltb, in_=lt, func=AF.Ln, bias=1.0)
        for h in range(1, 4):
            proj_v(h)
            yield
        cum = IDB if smp else TRIB
        for p in range(2):
            bkb = bank()
            for t in range(nt):
                MM(P[bkb][:, t * ts:(t + 1) * ts], LTOKB[0:ts, t, p * 128:(p + 1) * 128], cum[0:ts, 0:ts], True, True,
                   ["LTOKB", "TRIB", "IDB", OVL], ("ps", bkb))
            A("act", "activation", [("ps", bkb), OVL], [K("EQ", p)], out=T["EQ"][:, p, 0:n], in_=P[bkb][:, 0:n], func=AF.Exp,
              scale=-1.0 / 16)
            A("act", "activation", [("ps", bkb), OVL], [K("EK", p)], out=T["EK"][:, p, 0:n], in_=P[bkb][:, 0:n], func=AF.Exp,
              scale=1.0 / 16)
        yield
        if not smp:
            if g.first:
                A("pool", "memset", [OVL], [K("UT")], T["UT"][:, :, 0:15], 0.0)
            else:
                A("pool", "tensor_copy", ["HALO", OVL], [K("UT")], out=T["UT"][:, :, 0:15], in_=HALO[:, l, :, :])
        for gi in range(4):
            proj(CU + gi * 128, 128, lambda bk, gi=gi: A("act", "copy", [("ps", bk), OVL], [K("UT")],
                                                         out=T["UT"][:, gi, 15:15 + n], in_=P[bk][:, 0:n]))
            yield
        flush()
        if not smp:
            A("pool", "tensor_copy", [K("UT"), OVL], ["HALO"], out=HALO[:, l, :, :], in_=T["UT"][:, :, n:n + 15])
        qo, ko = (T["QT32"], T["KT32"]) if smp else (T["QT"], T["KT"])
        for p in range(2):
            proj(CQ + p * 128, 128, lambda bk, p=p: A("dve", "scalar_tensor_tensor", [("ps", bk), K("EQ", p), OVL], [K("QT")],
                                                      out=qo[:, p, 0:n], in0=P[bk][:, 0:n], scalar=0.125,
                                                      in1=T["EQ"][:, p, 0:n], op0=ALU.mult, op1=ALU.mult))
            yield
            if smp:
                flush()
                A("dve", "tensor_copy", [K("QT"), OVL], [K("QTB")], out=QTB[:, p, 0:n], in_=qo[:, p, 0:n])
                def ev(bk, p=p):
                    A("dve", "tensor_copy", [("ps", bk), OVL], [K("K32", p)], out=T["K32"][:, p, 0:n], in_=P[bk][:, 0:n])
                    A("dve", "tensor_tensor", [K("K32", p), K("EK", p), OVL], [K("KT")], out=ko[:, p, 0:n],
                      in0=T["K32"][:, p, 0:n], in1=T["EK"][:, p, 0:n], op=ALU.mult)
                proj(CK + p * 128, 128, ev)
            else:
                proj(CK + p * 128, 128, lambda bk, p=p: A("dve", "tensor_tensor", [("ps", bk), K("EK", p), OVL], [K("KT")],
                                                          out=ko[:, p, 0:n], in0=P[bk][:, 0:n], in1=T["EK"][:, p, 0:n],
                                                          op=ALU.mult))
            yield
        for h in range(4):
            proj(CG + h * 128, 128, lambda bk, h=h: A("act", "activation", [("ps", bk), OVL], [K("SG")],
                                                      out=T["SG"][:, h, 0:n], in_=P[bk][:, 0:n], func=AF.Silu))
            yield
        flush()
        yield

    def head_norm(l, g, T, mc, B):
        n = g.n
        si = T["i"]
        pv = l * PV_L
        tg = B["tag"]
        fine = B.get("fine", False)
        OT_, OSQ_, RR_, MX_ = B["OT"], B["OSQ"], B["RR"], B["MIXT"]
        K = lambda nm, *r: (nm, si) + r
        for h in range(4):
            sb = h % 2
            A("act", "activation", [("OT", tg), OVL], [("OSQ", tg, sb)], out=OSQ_[:, sb, 0:n], in_=OT_[:, h, 0:n], func=AF.Square)
            if fine:
                yield
            bk = bank()
            MM(P[bk][:, 0:n], ONB[:], OSQ_[:, sb, 0:n], True, True, [("OSQ", tg, sb), "ONB", OVL], ("ps", bk))
            if fine:
                yield
                yield
            A("act", "activation", [("ps", bk), "CON", OVL], [("RR", tg, sb)], out=RR_[:, sb, 0:n], in_=P[bk][:, 0:n], func=AF.Ln,
              scale=1.0 / 128, bias=eps_ap)
            A("act", "activation", [("RR", tg, sb), OVL], [("RR", tg, sb)], out=RR_[:, sb, 0:n], in_=RR_[:, sb, 0:n], func=AF.Exp,
              scale=-0.5)
            if fine:
                yield
            A("dve", "scalar_tensor_tensor", [("OT", tg), ("RR", tg, sb), "PV", OVL], [("RR", tg, sb)], out=RR_[:, sb, 0:n],
              in0=OT_[:, h, 0:n], scalar=PV[:, pv + PV_GG:pv + PV_GG + 1], in1=RR_[:, sb, 0:n], op0=ALU.mult, op1=ALU.mult)
            A("dve", "tensor_tensor", [("RR", tg, sb), K("SG"), OVL], [("MIXT", tg, mc)], out=MX_[:, h, mc:mc + n], in0=RR_[:, sb, 0:n],
              in1=T["SG"][:, h, 0:n], op=ALU.mult)
            yield

    def part2a(l, g, T, mc, B):
        n = g.n
        pv = l * PV_L
        smp = g.sample
        if smp:
            yield from gla_sample(l, g, T, B)
            yield from pool_sample(l, g, T, B)
        else:
            pool_prompt(l, g, T)
            yield
            yield from gla_prompt(l, g, T)
        assert smp or not mixt_busy[0], "MIXT would be overwritten before the previous pair's out-projection was emitted"
        yield from head_norm(l, g, T, mc, B)
        def pm_evac(gi, bk):
            A("act", "activation", [("ps", bk), "PV", OVL], [("MIXT", B["tag"], mc)], out=B["MIXT"][:, 4 + gi, mc:mc + n], in_=P[bk][:, 0:n],
              func=AF.Copy, scale=PV[:, pv + PV_PS + gi:pv + PV_PS + gi + 1])

        prev = None
        for gi in range(4):
            bk = bank()
            if smp:
                MM(P[bk][:, 0:n], PW[:, gi, :], B["PD"][:, gi, 0:n], True, True, ["PW", ("PD", B["tag"]), OVL], ("ps", bk))
                yield
                yield
                pm_evac(gi, bk)
            else:
                MM(P[bk][:, 0:n], PWS[:, gi, :], PD[:, gi, 0:n], True, False, ["PWS", ("PD", "p"), OVL], ("ps", bk))
                MM(P[bk][:, 0:n], PWN[:, gi, :], UB[:, gi, 0:n], False, True, ["PWS", "UB", OVL], ("ps", bk))
                if prev is not None:
                    pm_evac(*prev)
                prev = (gi, bk)
        if prev is not None:
            yield
            pm_evac(*prev)
        yield

    mixt_busy = [False]

    def part2b(l, keys, cols, n, mcs, B, which):
        pv = l * PV_L
        MX_ = B["MIXT"]
        if B["tag"] == "p":
            mixt_busy[0] = True
        xks = [("X", k) for k in keys]
        hks = [("H", k) for k in keys]
        mks = [("MIXT", B["tag"], m) for m in mcs]
        prev = None

        def add(dc, bk):
            A("dve", "tensor_tensor", [("ps", bk)] + xks, xks, out=X[:, dc, cols], in0=P[bk][:, 0:n], in1=X[:, dc, cols],
              op=ALU.add)

        for dc in range(8):
            bk = bank()
            for kc in range(8):
                MM(P[bk][:, 0:n], Wout[:, kc, dc * 128:(dc + 1) * 128], MX_[:, kc, 0:n], kc == 0, kc == 7,
                   ["Wout", OVL] + mks, ("ps", bk))
            if B.get("fine", False):
                yield
                yield
                add(dc, bk)
            else:
                if prev is not None:
                    add(*prev)
                prev = (dc, bk)
            if dc == 7 and B["tag"] == "p":
                mixt_busy[0] = False
            yield
        if prev is not None:
            add(*prev)
            yield
        yield from norm(cols, n, xks, pv + PV_G2, lambda c: (H[:, c, cols], hks, None), which)

    sbf_cur = [0]

    def gla_prompt(l, g, T):
        si = T["i"]
        K = lambda nm, *r: (nm, si) + r
        EQ, KT, QT, VT = T["EQ"], T["KT"], T["QT"], T["VT"]
        if g.first:
            A("dve", "memset", [], [("SST", l)], SST[:, l, :, :], 0.0)
        if g.key == 0:
            A("act", "copy", [("SST", l)], [("SBF", sbf_cur[0])], out=SBF[:, sbf_cur[0], :, :], in_=SST[:, l, :, :])
        for t in range(g.nt):
            tc_ = slice(t * 128, (t + 1) * 128)
            last = t * 128 + 127
            sc = sbf_cur[0]
            sn = 1 - sc
            sbf_cur[0] = sn
            tb = t % 2
            for p in range(2):
                A("dve", "tensor_scalar", [K("KT"), K("EQ", p), OVL], ["KE"], out=KE[:, p, tc_], in0=KT[:, p, tc_],
                  scalar1=EQ[:, p, last:last + 1], scalar2=None, op0=ALU.mult)
            bab = [bank(), bank()]
            for h in range(4):
                p, hh, r = h // 2, h % 2, slice((h % 2) * 64, (h % 2) * 64 + 64)
                MM(P[bab[hh]][:, p * 128:(p + 1) * 128], KT[r, p, tc_], QT[r, p, tc_], True, True, [K("KT"), K("QT"), OVL],
                   ("ps", bab[hh]))
            yield
            pb = bbank()
            for p in range(2):
                TR(PB[pb][:, p * 128:(p + 1) * 128], KE[:, p, tc_], IDB[:], ["KE", "IDB", OVL], ("pb", pb))
            for h in range(4):
                TR(PB[pb][:, 256 + h * 128:256 + (h + 1) * 128], VT[:, h, tc_], IDB[:], [K("VT"), "IDB", OVL], ("pb", pb))
            for hh in range(2):
                A("dve", "tensor_tensor", [("ps", bab[hh]), "CON", OVL], [("ATTM", tb)],
                  out=ATTM[:, tb, hh * 256:(hh + 1) * 256].rearrange("p (h c) -> p h c", h=2),
                  in0=P[bab[hh]][:, 0:256].rearrange("p (h c) -> p h c", h=2),
                  in1=triU.unsqueeze(1).to_broadcast([128, 2, 128]), op=ALU.mult)
            yield
            A("act", "copy", [("pb", pb), OVL], [("TOK", tb)], out=TOK[:, tb, :], in_=PB[pb][:, 0:768])
            yield
            bd = bank()
            for p in range(2):
                MM(P[bd][:, p * 256:(p + 1) * 256], TOK[:, tb, p * 128:(p + 1) * 128],
                   TOK[:, tb, 256 + 2 * p * 128:256 + (2 * p + 2) * 128], True, True, [("TOK", tb), OVL], ("ps", bd))
            bo = bank()
            for h in range(4):
                p, r = h // 2, slice((h % 2) * 64, (h % 2) * 64 + 64)
                o = P[bo][:, h * 128:(h + 1) * 128]
                ai = (h % 2) * 2 + h // 2
                MM(o, TOK[:, tb, 256 + h * 128:256 + (h + 1) * 128], ATTM[:, tb, ai * 128:(ai + 1) * 128], True, False,
                   [("TOK", tb), ("ATTM", tb), OVL], ("ps", bo))
                MM(o, SBF[r, sc, p, :], QT[r, p, tc_], False, True, [("SBF", sc), K("QT"), OVL], ("ps", bo))
            yield
            for p in range(2):
                for hh in range(2):
                    r = slice(hh * 64, hh * 64 + 64)
                    A("dve", "scalar_tensor_tensor", [("ps", bd), K("EQ", p), ("SST", l), OVL], [("SST", l)],
                      out=SST[r, l, p, :], in0=SST[r, l, p, :], scalar=EQ[r, p, last:last + 1],
                      in1=P[bd][r, p * 256 + hh * 128:p * 256 + (hh + 1) * 128], op0=ALU.mult, op1=ALU.add)
            A("act", "copy", [("ps", bo), OVL], [("OT", "p")], out=OT[:, :, tc_],
              in_=P[bo][:, :].rearrange("p (h c) -> p h c", h=4))
            yield
            A("act", "copy", [("SST", l), OVL], [("SBF", sn)], out=SBF[:, sn, :, :], in_=SST[:, l, :, :])
            yield

    def gla_sample(l, g, T, B):
        n = NS
        si = T["i"]
        K = lambda nm, *r: (nm, si) + r
        EQ, QT32, KT32, VT32, K32 = T["EQ"], T["QT32"], T["KT32"], T["VT32"], T["K32"]
        S0t = S0[0]

        def load(bh):
            for p in range(2):
                src = sgla_d[l, bh * 8:(bh + 1) * 8].rearrange("b h k v -> (h k) b v")[p * 128:(p + 1) * 128]
                DMA("sp", S0t[:, :, p, :], src, [OVL], [("S0", 0)], "s0_0", cont=(p == 1))

        load(0)
        yield
        bt = bank()
        for p in range(2):
            TR(P[bt][0:NS, p * 128:(p + 1) * 128], K32[:, p, 0:n], ident32, [K("K32", p), "CON", OVL], ("ps", bt))
        yield
        A("dve", "tensor_copy", [("ps", bt), OVL], ["TOKS"], out=TOKS[0:NS, 0:256], in_=P[bt][0:NS, 0:256])
        bt = bank()
        for h in range(4):
            TR(P[bt][0:NS, h * 128:(h + 1) * 128], VT32[:, h, 0:n], ident32, [K("VT32"), "CON", OVL], ("ps", bt))
        yield
        A("dve", "tensor_copy", [("ps", bt), OVL], ["TOKS"], out=TOKS[0:NS, 256:768], in_=P[bt][0:NS, 0:512])
        A("dve", "tensor_tensor", [K("QT"), K("KT"), OVL], ["PROD"], out=PROD[:, :, :], in0=QT32[:, :, :], in1=KT32[:, :, :],
          op=ALU.mult)
        yield
        bqb = [bank(), bank()]
        for h in range(4):
            p, hh, r = h // 2, h % 2, slice((h % 2) * 64, (h % 2) * 64 + 64)
            MM(P[bqb[hh]][:, p * NS:(p + 1) * NS], ones32[r, :], PROD[r, p, :], True, True, ["CON", "PROD", OVL],
               ("ps", bqb[hh]))
        yield
        for h in range(4):
            p, hh = h // 2, h % 2
            A("dve", "tensor_tensor", [("ps", bqb[hh]), K("VT32"), OVL], ["TMPS"], out=TMPS[:, h, :],
              in0=P[bqb[hh]][:, p * NS:(p + 1) * NS], in1=VT32[:, h, :], op=ALU.mult)
        yield
        for bh in range(2):
            if bh == 1:
                load(1)
                yield
                yield
                yield
            A("act", "copy", [("S0", 0), OVL], ["S0BF"], out=S0BF[:, :, :, :], in_=S0t[:, :, :, :])
            yield
            yield
            bob = [bank(), bank()]
            for bl in range(8):
                b = bh * 8 + bl
                for h in range(4):
                    p, hh, r = h // 2, h % 2, slice((h % 2) * 64, (h % 2) * 64 + 64)
                    MM(P[bob[hh]][:, p * NS + b:p * NS + b + 1], S0BF[r, bl, p, :], QTB[r, p, b:b + 1], True, True,
                       ["S0BF", K("QTB"), OVL], ("ps", bob[hh]))
            yield
            yield
            for h in range(4):
                p, hh = h // 2, h % 2
                A("act", "copy", [("ps", bob[hh]), OVL], ["OIN"], out=OIN[:, h, bh * 8:(bh + 1) * 8],
                  in_=P[bob[hh]][:, p * NS + bh * 8:p * NS + bh * 8 + 8])
            yield
            for bq in range(4):
                A("dve", "tensor_tensor", ["TOKS", "IDB", OVL], ["KMASK"], out=KMASK[0:NS, :, :],
                  in0=TOKS[0:NS, 0:256].unsqueeze(1).to_broadcast([NS, 2, 256]),
                  in1=IDB[0:NS, bh * 8 + bq * 2:bh * 8 + bq * 2 + 2].unsqueeze(2).to_broadcast([NS, 2, 256]), op=ALU.mult)
                yield
                bds = []
                for bl2 in range(2):
                    bd = bank()
                    bds.append(bd)
                    for p in range(2):
                        MM(P[bd][:, p * 256:(p + 1) * 256], KMASK[0:NS, bl2, p * 128:(p + 1) * 128],
                           TOKS[0:NS, 256 + 2 * p * 128:256 + (2 * p + 2) * 128], True, True, ["KMASK", "TOKS", OVL], ("ps", bd))
                yield
                yield
                for bl2 in range(2):
                    bl = bq * 2 + bl2
                    b = bh * 8 + bl
                    bd = bds[bl2]
                    for p in range(2):
                        for hh in range(2):
                            r = slice(hh * 64, hh * 64 + 64)
                            A("dve", "scalar_tensor_tensor", [("ps", bd), K("EQ", p), ("S0", 0), OVL], [("S0", 0)],
                              out=S0t[r, bl, p, :], in0=S0t[r, bl, p, :], scalar=EQ[r, p, b:b + 1],
                              in1=P[bd][r, p * 256 + hh * 128:p * 256 + (hh + 1) * 128], op0=ALU.mult, op1=ALU.add)
                    yield
            DMA("sp", glas_d[l][:, bh * 8:(bh + 1) * 8, :, :], S0t[:], [("S0", 0), OVL], [], "s0o_0")
            yield
        for h in range(4):
            A("dve", "tensor_tensor", ["OIN", "TMPS", OVL], [("OT", B["tag"])], out=B["OT"][:, h, 0:n],
              in0=OIN[:, h, :], in1=TMPS[:, h, :], op=ALU.add)
        yield

    def pool_prompt(l, g, T):
        n = g.n
        L = n + 15
        si = T["i"]
        UT = T["UT"]
        uk = ("UT", si)
        A("dve", "tensor_copy", [uk, OVL], ["UB"], out=UB[:, :, 0:n], in_=UT[:, :, 15:L])
        for gi, w in enumerate(POOL_W):
            src = UT[:, gi, :]
            steps = []
            sh = 1
            while sh < w:
                steps.append(sh)
                sh *= 2
            starts = [15]
            for s_ in reversed(steps):
                starts.append(starts[-1] - s_)
            starts = list(reversed(starts))
            cur_ap = src
            for i, s_ in enumerate(steps):
                a0 = starts[i + 1]
                lastlvl = (i == len(steps) - 1)
                if lastlvl and not g.first:
                    A("pool", "tensor_tensor", [uk, ("PT", 0), ("PT", 1), OVL], [("PD", "p")], out=PD[:, gi, 0:n],
                      in0=cur_ap[:, 15:L], in1=cur_ap[:, 15 - s_:L - s_], op=ALU.add)
                else:
                    dst = PT[:, i % 2, :]
                    A("pool", "tensor_tensor", [uk, ("PT", 0), ("PT", 1), OVL], [("PT", i % 2)], out=dst[:, a0:L],
                      in0=cur_ap[:, a0:L], in1=cur_ap[:, a0 - s_:L - s_], op=ALU.add)
                    cur_ap = dst
            if g.first:
                k = w - 1
                A("pool", "tensor_copy", [("PT", 0), ("PT", 1), OVL], [("PD", "p")], out=PD[:, gi, 0:n], in_=cur_ap[:, 15:L])
                A("pool", "tensor_tensor", [("PT", 0), ("PT", 1), "CON", OVL], [("PD", "p")], out=PD[:, gi, 0:k],
                  in0=cur_ap[:, 15:15 + k], in1=CON[:, C_WIC + gi * 15:C_WIC + gi * 15 + k], op=ALU.mult)

    def pool_sample(l, g, T, B):
        n = NS
        si = T["i"]
        UT = T["UT"]
        uk = ("UT", si)
        DMA("sp", pools_old_d[l], spool_d[l, :, 1:15, :], [], [], "psold")
        DMA("sp", pools_new_d[l], UT[:, :, 15:15 + n], [uk, OVL], [], "psnew")
        bk = None
        for bh in range(2):
            DMA("sp", XPOOL[:], spool_d[l, bh * 8:(bh + 1) * 8].rearrange("b r c -> (b r) c"), [OVL], ["XPOOL"], "xpool")
            yield
            yield
            if bk is None:
                bk = bank()
            for gi in range(4):
                MM(P[bk][:, gi * NS + bh * 8:gi * NS + bh * 8 + 8], XPOOL[:, gi * 128:(gi + 1) * 128],
                   CON[0:120, C_WSEL + gi * 8:C_WSEL + gi * 8 + 8], True, True, ["XPOOL", "CON", OVL], ("ps", bk))
            yield
        yield
        for gi, w in enumerate(POOL_W):
            A("dve", "scalar_tensor_tensor", [("ps", bk), uk, OVL], [("PD", B["tag"])], out=B["PD"][:, gi, 0:n], in0=UT[:, gi, 15:15 + n],
              scalar=1.0 / w - 1.0, in1=P[bk][:, gi * NS:(gi + 1) * NS], op0=ALU.mult, op1=ALU.add)
        yield

    def mlp(l, groups, nxt, prefetch_only=False):
        blocks = []
        for fh in range(2):
            for fb in range(4):
                blocks.append(("up", fh, fb))
            for db in range(4):
                blocks.append(("dn", fh, db))

        def bufof(i):
            return (WSX, ("WSX", 0), "wsx") if i == 0 else (WS[(i - 1) % 4], ("WS", (i - 1) % 4), "ws%d" % ((i - 1) % 4))

        def issue(i):
            kind, fh, j = blocks[i]
            wt, wkey, slot = bufof(i)
            if kind == "up":
                c0 = fh * 2048 + j * 512
                src = wup_d[l].rearrange("(k p) f -> p k f", p=128)[:, :, c0:c0 + 512]
                dst = wt[:, :].rearrange("p (k f) -> p k f", k=8)
            else:
                src = wdn_d[l, fh * 2048:(fh + 1) * 2048, j * 256:(j + 1) * 256].rearrange("(c p) d -> p c d", p=128)
                dst = wt[:, :].rearrange("p (c d) -> p c d", c=16)
            DMA("pool", dst, src, [] if i == 0 else [OVL], [wkey], slot)

        if prefetch_only:
            issue(0)
            return
        for i in range(1, 4):
            issue(i)
        rlb = [0]
        for i, (kind, fh, j) in enumerate(blocks):
            if i >= 1 and i + 3 < len(blocks):
                issue(i + 3)
            if i == 11 and nxt is not None:
                load_layer_weights(nxt)
            wt, wkey, _ = bufof(i)
            if kind == "up":
                wv = wt[:, :].rearrange("p (k f) -> p k f", k=8)
                for jj in range(4):
                    fc = j * 4 + jj
                    for keys, cols, n in groups:
                        bk = bank()
                        hks = [("H", k) for k in keys]
                        for kc in range(8):
                            MM(P[bk][:, 0:n], wv[:, kc, jj * 128:(jj + 1) * 128], H[:, kc, cols], kc == 0, kc == 7,
                               [wkey] + hks + [OVL], ("ps", bk))
                        rb = rlb[0]
                        rlb[0] = 1 - rb
                        A("act", "activation", [("ps", bk), OVL], [("RL", rb)], out=RL[:, rb, 0:n], in_=P[bk][:, 0:n],
                          func=AF.Relu)
                        A("dve", "tensor_tensor", [("RL", rb), OVL], [("AT", keys[0])], out=AT[:, fc, cols], in0=RL[:, rb, 0:n],
                          in1=RL[:, rb, 0:n], op=ALU.mult)
            else:
                wv = wt[:, :].rearrange("p (c d) -> p c d", c=16)
                for dl in range(2):
                    dc = j * 2 + dl
                    for keys, cols, n in groups:
                        bk = bank()
                        xks = [("X", k) for k in keys]
                        for fc in range(16):
                            MM(P[bk][:, 0:n], wv[:, fc, dl * 128:(dl + 1) * 128], AT[:, fc, cols], fc == 0, fc == 15,
                               [wkey, ("AT", keys[0]), OVL], ("ps", bk))
                        A("dve", "tensor_tensor", [("ps", bk)] + xks, xks, out=X[:, dc, cols], in0=P[bk][:, 0:n],
                          in1=X[:, dc, cols], op=ALU.add)

    def phase_switch():
        A("dve", "memset", [], [OVL], RL[:, 0, 0:8], 0.0)

    STAT = {'rounds': 0, 'bgsteps': 0, 'bgtail': 0}
    order = [(sg, l) for sg in range(nsg) for l in range(depth)]
    load_layer_weights(0)
    for sg in range(nsg):
        col0 = sg * 1024
        ncols = 1024 if sg == 0 else NT
        grps = [Grp(i, i * GM, GM, False, col0 + i * GM, sg == 0 and i == 0) for i in range(4)]
        if sg == 1:
            grps.append(Grp("s", 1024, NS, True, 2048, False))
        xkeys = [("X", g.key) for g in grps]
        xsrc = xT_d.rearrange("(c p) t -> p c t", p=128)
        DMA("sp", X[:, :, 0:512], xsrc[:, :, col0:col0 + 512], [], [("X", 0), ("X", 1)], "x0")
        DMA("sp", X[:, :, 512:1024], xsrc[:, :, col0 + 512:col0 + 1024], [], [("X", 2), ("X", 3)], "x1")
        if sg == 1:
            DMA("sp", X[:, :, 1024:NT], xsrc[:, :, 2048:NTOK], [], [("X", "s")], "x2")
        mgroups = [([0, 1], slice(0, 512), 512), ([2, 3], slice(512, 1024), 512)]
        if sg == 1:
            mgroups = [([0, 1], slice(0, 347), 347), ([1, 2], slice(347, 694), 347), ([2, 3, "s"], slice(694, 1040), 346)]
        for l in range(depth):
            idx = order.index((sg, l))
            nxt = order[idx + 1][1] if idx + 1 < len(order) else None
            for gi, w in enumerate(POOL_W):
                A("act", "activation", ["PW"], ["PWS"], out=PWS[:, gi, :], in_=PW[:, gi, :], func=AF.Copy, scale=1.0 / w)
            A("act", "activation", ["PW"], ["PWS"], out=PWN[:, :, :], in_=PW[:, :, :], func=AF.Copy, scale=-1.0)
            def rr_run(iters, bg=None):
                live = [x if isinstance(x, tuple) else (x, 1) for x in iters]
                while live:
                    for ent in list(live):
                        it, k = ent
                        for _ in range(k):
                            try:
                                next(it)
                            except StopIteration:
                                live.remove(ent)
                                break
                    STAT["rounds"] += 1
                    for _ in range(1):
                        if bg is not None and bg[0] is not None:
                            try:
                                next(bg[0])
                                STAT["bgsteps"] += 1
                            except StopIteration:
                                bg[0] = None

            PB2 = dict(OT=OT, OSQ=OSQ, RR=RR, PD=PD, MIXT=MIXT, tag="p")
            SB2 = dict(OT=OTs, OSQ=OSQs, RR=RRs, PD=PDs, MIXT=MIXS, tag="s", fine=True)
            pairs = [([0, 1], slice(0, 512), 512, [0, GM], PB2, 2), ([2, 3], slice(512, 1024), 512, [0, GM], PB2, 2)]
            pg = [g for g in grps if not g.sample]
            bg = [None]
            if sg == 1:
                gs = grps[-1]
                rr_run([part1(l, gs, SS)])

                def bg_gen(l=l, gs=gs):
                    yield from part2a(l, gs, SS, 0, SB2)
                    yield from part2b(l, ["s"], slice(1024, 1040), NS, [0], SB2, 3)

                bg[0] = bg_gen()
            load_wout(l)
            its1 = [part1(l, g, SETS[i % 2]) for i, g in enumerate(pg)]
            rr_run([its1[0]], bg)
            mlp(l, mgroups, None, prefetch_only=True)
            pending_b = None
            for i, g in enumerate(pg):
                mc = 0 if i % 2 == 0 else GM
                iters = [part2a(l, g, SETS[i % 2], mc, PB2)]
                if i + 1 < len(pg):
                    iters.append(its1[i + 1])
                if pending_b is not None:
                    iters.insert(0, (part2b(l, *pending_b), 2))
                    pending_b = None
                rr_run(iters, bg)
                if i % 2 == 1:
                    pending_b = pairs[i // 2]
            rr_run([part2b(l, *pending_b)], bg)
            while bg[0] is not None:
                rr_run([], bg) if False else None
                try:
                    next(bg[0])
                    STAT["bgtail"] += 1
                except StopIteration:
                    bg[0] = None
            if sg == 1:
                DMA("sp", glap_d[l], SST[:, l, :, :], [("SST", l)], [], "glap")
                DMA("sp", poolp_d[l], HALO[:, l, :, :], ["HALO"], [], "poolp")
            phase_switch()
            mlp(l, mgroups, nxt)
            phase_switch()
        OTS = OT[:, :, :].rearrange("p (a h) c -> p a (h c)", a=2)
        fpairs = [([0, 1], slice(0, 512), 512, col0), ([2, 3], slice(512, 1024), 512, col0 + 512)]
        if sg == 1:
            fpairs.append((["s"], slice(1024, 1040), NS, 2048))
        for keys, cols, n, gcol0 in fpairs:
            def out_fn(c, n=n, gcol0=gcol0):
                yb_ = c % 2
                def post():
                    DMA("sp", yT_d[c * 128:(c + 1) * 128, gcol0:gcol0 + n], OTS[:, yb_, 0:n], [("YS", yb_), OVL], [],
                        "yo%d" % yb_)
                return OTS[:, yb_, 0:n], [("YS", yb_), ("OT", "p")], post
            for _ in norm(cols, n, [("X", k) for k in keys], PV_FIN, out_fn, 2):
                pass
        phase_switch()

    print('n_ops', len(S.ops), STAT)
    if maxops is not None:
        S.ops = S.ops[:maxops]
    S.emit(nc, st)
    st.close()
    return nc


_NC_CACHE = {}


def _host_consts():
    con = np.zeros((128, NCON), np.float32)
    con[:, C_ID:C_ID + 128] = np.eye(128, dtype=np.float32)
    con[:, C_TRI:C_TRI + 128] = np.triu(np.ones((128, 128), np.float32))
    con[:, C_ONE:C_ONE + 128] = 1.0
    con[:, C_INVC:C_INVC + 15] = (1.0 / np.arange(1, 16, dtype=np.float64)).astype(np.float32)[None, :]
    con[:, C_EPS] = EPS
    for gi, w in enumerate(POOL_W):
        for b in range(8):
            for r in range(15 - (w - 1), 15):
                con[b * 15 + r, C_WSEL + gi * 8 + b] = 1.0 / w
    for gi, w in enumerate(POOL_W):
        con[:, C_WIC + gi * 15:C_WIC + gi * 15 + 15] = (w / np.arange(1, 16, dtype=np.float64)).astype(np.float32)[None, :]
    return con


def kernel(x_prompt, x_sample, state_gla, state_pool, norm1_g, w_in, w_gate, b_gate, gla_norm_g, pool_w,
           pool_scale, w_out, norm2_g, w_up, w_down, final_g):
    f = lambda a: np.ascontiguousarray(np.asarray(a, dtype=np.float32))
    x_prompt, x_sample, state_gla, state_pool = f(x_prompt), f(x_sample), f(state_gla), f(state_pool)
    w_in, w_out, w_up, w_down = f(w_in), f(w_out), f(w_up), f(w_down)
    pool_w, w_gate, b_gate = f(pool_w), f(w_gate), f(b_gate)
    pvec = np.zeros((128, NPV), np.float32)
    n1, n2, gg, ps, fg = f(norm1_g), f(norm2_g), f(gla_norm_g), f(pool_scale), f(final_g)
    for l in range(DEPTH):
        b = l * PV_L
        pvec[:, b + PV_G1:b + PV_G1 + 8] = n1[l].reshape(8, 128).T
        pvec[:, b + PV_G2:b + PV_G2 + 8] = n2[l].reshape(8, 128).T
        pvec[:, b + PV_GG] = gg[l]
        pvec[:, b + PV_PS:b + PV_PS + 4] = ps[l].reshape(4, 128).T
    pvec[:, PV_FIN:PV_FIN + 8] = fg.reshape(8, 128).T
    con = _host_consts()
    if "nc" not in _NC_CACHE:
        _NC_CACHE["nc"] = build_program()
    nc = _NC_CACHE["nc"]
    in_maps = []
    for c in range(NCORE):
        xT = np.empty((D, NTOK), np.float32)
        xT[:, :SEQ] = x_prompt[c].T
        xT[:, SEQ:] = x_sample[c * NS:(c + 1) * NS, 0, :].T
        in_maps.append({
            "xT": xT,
            "sgla": np.ascontiguousarray(state_gla[:, c * NS:(c + 1) * NS]),
            "spool": np.ascontiguousarray(state_pool[:, c * NS:(c + 1) * NS]),
            "w_in": w_in, "w_out": w_out, "w_up": w_up, "w_down": w_down,
            "pool_w": pool_w, "w_gate": w_gate, "b_gate": b_gate, "pvec": pvec, "consts": con,
        })
    res = run_bass_kernel_spmd(nc, in_maps, core_ids=list(range(NCORE)))
    B = NCORE
    y_prompt = np.empty((B, SEQ, D), np.float32)
    y_sample = np.empty((B * NS, 1, D), np.float32)
    gla_p = np.empty((DEPTH, B, 4, 64, 128), np.float32)
    pool_p = np.empty((DEPTH, B, 15, 512), np.float32)
    gla_s = np.empty((DEPTH, B * NS, 4, 64, 128), np.float32)
    pool_s = np.empty((DEPTH, B * NS, 15, 512), np.float32)
    for c in range(NCORE):
        r = res.results[c]
        yT = np.asarray(r["yT"])
        y_prompt[c] = yT[:, :SEQ].T
        y_sample[c * NS:(c + 1) * NS, 0, :] = yT[:, SEQ:].T
        gp = np.asarray(r["gla_p"]).reshape(DEPTH, 2, 64, 2, 128)
        gla_p[:, c] = gp.transpose(0, 3, 1, 2, 4).reshape(DEPTH, 4, 64, 128)
        pp = np.asarray(r["pool_pT"])
        pool_p[:, c] = pp.transpose(0, 3, 2, 1).reshape(DEPTH, 15, 512)
        gs = np.asarray(r["gla_s"]).reshape(DEPTH, 2, 64, NS, 2, 128)
        gla_s[:, c * NS:(c + 1) * NS] = gs.transpose(0, 3, 4, 1, 2, 5).reshape(DEPTH, NS, 4, 64, 128)
        pool_s[:, c * NS:(c + 1) * NS, 0:14] = np.asarray(r["pool_s_old"])
        pn = np.asarray(r["pool_s_newT"])
        pool_s[:, c * NS:(c + 1) * NS, 14] = pn.transpose(0, 3, 2, 1).reshape(DEPTH, NS, 512)
    return (y_prompt, y_sample, gla_p, pool_p, gla_s, pool_s)
```

```python
import numpy as np
from contextlib import ExitStack
import concourse.bass as bass
import concourse.mybir as mybir
from concourse.bass_utils import run_bass_kernel_spmd

F32 = mybir.dt.float32
BF16 = mybir.dt.bfloat16
ALU = mybir.AluOpType
AF = mybir.ActivationFunctionType

QUEUES = ("pe", "act", "dve", "pool", "sp")

D = 1024
DEPTH = 4
NCORE = 8
SEQ = 2048
NS = 16
NTOK = SEQ + NS
IN_W = 2064
DFF = 4096
NT = 1040
GM = 256
EPS = 1e-6
POOL_W = (2, 4, 8, 16)
CQ, CK, CV, CG, CA, CU = 0, 256, 512, 1024, 1536, 1552
C_ID, C_TRI, C_ONE, C_INVC, C_EPS, C_WSEL, C_WIC, NCON = 0, 128, 256, 384, 399, 400, 432, 492
PV_L, PV_G1, PV_G2, PV_GG, PV_PS, PV_FIN, NPV = 21, 0, 8, 16, 17, 84, 92


class Op:
    __slots__ = ("q", "fn", "deps", "slot", "fill", "sig", "waits", "known", "signals", "idx")


class Fill:
    __slots__ = ("last",)


class Sched:
    def __init__(self, same_engine_sync=True):
        self.ops = []
        self.last_writer = {}
        self.readers = {}
        self.same_engine_sync = same_engine_sync
        self.slots = []
        self.slot_fill = {}

    def add(self, q, fn, reads=(), writes=(), slot=None, cont=False):
        op = Op()
        op.q = q
        op.fn = fn
        op.slot = slot
        op.idx = len(self.ops)
        op.fill = None
        deps = {}
        lw, rd = self.last_writer, self.readers
        for r in reads:
            w = lw.get(r)
            if w is not None:
                deps[w.idx] = w
        for r in writes:
            w = lw.get(r)
            if w is not None:
                deps[w.idx] = w
            for x in rd.get(r, ()):
                deps[x.idx] = x
        for r in reads:
            rd.setdefault(r, []).append(op)
        for r in writes:
            lw[r] = op
            rd[r] = []
        if slot is not None:
            if slot not in self.slot_fill:
                self.slots.append(slot)
            prev = self.slot_fill.get(slot)
            if cont and prev is not None:
                op.fill = prev
            else:
                if prev is not None:
                    deps[prev.last.idx] = prev.last
                op.fill = Fill()
                self.slot_fill[slot] = op.fill
            op.fill.last = op
        deps.pop(op.idx, None)
        op.deps = list(deps.values())
        op.signals = False
        self.ops.append(op)
        return op

    def _src(self, op):
        return ("slot", op.slot) if op.slot is not None else ("q", op.q)

    def _skip(self, d, op):
        return (d.slot is None and op.slot is None and d.q == op.q and
                (d.q == "pe" or not self.same_engine_sync))

    def finalize(self):
        for op in self.ops:
            for d in op.deps:
                if self._skip(d, op):
                    continue
                d.signals = True
        counts = {}
        for op in self.ops:
            if op.slot is not None:
                op.signals = True
            if op.signals:
                s = self._src(op)
                inc = 16 if op.slot is not None else 1
                counts[s] = counts.get(s, 0) + inc
                op.sig = (s, counts[s], inc)
            else:
                op.sig = None
        self.final_counts = counts
        known = {q: {} for q in QUEUES}
        for op in self.ops:
            kq = known[op.q]
            need = {}
            for d in op.deps:
                if self._skip(d, op):
                    continue
                if d.slot is not None and d.fill is not op.fill:
                    d = d.fill.last
                    assert d.idx < op.idx, "consumer precedes end of DMA fill"
                s, c, _ = d.sig
                if kq.get(s, 0) >= c:
                    continue
                if need.get(s, (0, None))[0] < c:
                    need[s] = (c, d)
            waits = []
            for s, (c, d) in sorted(need.items(), key=lambda kv: -kv[1][1].idx):
                if kq.get(s, 0) >= c:
                    continue
                waits.append((s, c))
                for s2, c2 in d.known.items():
                    if kq.get(s2, 0) < c2:
                        kq[s2] = c2
                kq[s] = c
            op.waits = waits
            snap = dict(kq)
            if op.sig is not None:
                s, c, _ = op.sig
                if snap.get(s, 0) < c:
                    snap[s] = c
            op.known = snap

    def emit(self, nc, stack, final_wait_queue="sp"):
        self.finalize()
        sems = {}
        for q in QUEUES:
            sems[("q", q)] = stack.enter_context(nc.semaphore("sq_" + q))
        for i, sl in enumerate(self.slots):
            sems[("slot", sl)] = stack.enter_context(nc.semaphore("sd%d" % i))
        by_q = {q: [] for q in QUEUES}
        for op in self.ops:
            by_q[op.q].append(op)
        final = list(self.final_counts.items())

        def run(eng, q):
            for op in by_q[q]:
                for s, c in op.waits:
                    eng.wait_ge(sems[s], c)
                ins = op.fn(eng)
                if op.sig is not None:
                    s, c, inc = op.sig
                    ins.then_inc(sems[s], inc)
            if q == final_wait_queue:
                for s, c in final:
                    eng.wait_ge(sems[s], c)

        block = stack.enter_context(nc.Block())

        @block.tensor
        def _(e):
            run(e, "pe")

        @block.scalar
        def _(e):
            run(e, "act")

        @block.vector
        def _(e):
            run(e, "dve")

        @block.gpsimd
        def _(e):
            run(e, "pool")

        @block.sync
        def _(e):
            run(e, "sp")


class Grp:
    def __init__(self, key, c0, n, sample, g0, first):
        self.key, self.c0, self.n, self.sample, self.g0, self.first = key, c0, n, sample, g0, first
        self.ts = 16 if sample else 128
        self.nt = n // self.ts
        self.cols = slice(c0, c0 + n)


def build_program(depth=DEPTH, nsg=2, maxops=None):
    nc = bass.Bass("TRN2", target_bir_lowering=False)
    dram = lambda n, sh, kind: nc.dram_tensor(n, sh, F32, kind=kind).ap()
    xT_d = dram("xT", [D, NTOK], "ExternalInput")
    sgla_d = dram("sgla", [DEPTH, NS, 4, 64, 128], "ExternalInput")
    spool_d = dram("spool", [DEPTH, NS, 15, 512], "ExternalInput")
    win_d = dram("w_in", [DEPTH, D, IN_W], "ExternalInput")
    wout_d = dram("w_out", [DEPTH, D, D], "ExternalInput")
    wup_d = dram("w_up", [DEPTH, D, DFF], "ExternalInput")
    wdn_d = dram("w_down", [DEPTH, DFF, D], "ExternalInput")
    poolw_d = dram("pool_w", [DEPTH, 4, 128, 128], "ExternalInput")
    wgate_d = dram("w_gate", [DEPTH, 16, 256], "ExternalInput")
    bgate_d = dram("b_gate", [DEPTH, 256], "ExternalInput")
    pvec_d = dram("pvec", [128, NPV], "ExternalInput")
    con_d = dram("consts", [128, NCON], "ExternalInput")
    yT_d = dram("yT", [D, NTOK], "ExternalOutput")
    glap_d = dram("gla_p", [DEPTH, 128, 2, 128], "ExternalOutput")
    poolp_d = dram("pool_pT", [DEPTH, 128, 4, 15], "ExternalOutput")
    glas_d = dram("gla_s", [DEPTH, 128, NS, 2, 128], "ExternalOutput")
    pools_old_d = dram("pool_s_old", [DEPTH, NS, 14, 512], "ExternalOutput")
    pools_new_d = dram("pool_s_newT", [DEPTH, 128, 4, NS], "ExternalOutput")

    S = Sched()
    st = ExitStack()

    SB_LO, SB_HI = 16512, 229344
    cur = [SB_LO]

    def alloc(name, shape, dt, at=None):
        size = int(np.prod(shape[1:])) * (2 if dt == BF16 else 4)
        size = (size + 63) // 64 * 64
        if at is None:
            off = cur[0]
            cur[0] += size
        else:
            off = at[0]
            at[0] += size
        return nc.alloc_sbuf_tensor_at(name, list(shape), dt, offset=off)

    X = alloc("X", [128, 8, NT], F32)
    H = alloc("H", [128, 8, NT], BF16)
    Win = alloc("Win", [128, 8, IN_W], BF16)
    Wout = alloc("Wout", [128, 8, D], BF16)
    CON = alloc("CON", [128, NCON], F32)
    PV = alloc("PV", [128, NPV], F32)
    IDB = alloc("IDB", [128, 128], BF16)
    ONB = alloc("ONB", [128, 128], BF16)
    TRIB = alloc("TRIB", [128, 128], BF16)
    SST = alloc("SST", [128, DEPTH, 2, 128], F32)
    HALO = alloc("HALO", [128, DEPTH, 4, 15], F32)
    SBF = alloc("SBF", [128, 2, 2, 128], BF16)
    PWS = alloc("PWS", [128, 4, 128], BF16)
    PWN = alloc("PWN", [128, 4, 128], BF16)
    WSX = alloc("WSX", [128, 4096], BF16)
    WG = alloc("WG", [16, 256], BF16)
    BG = alloc("BG", [1, 256], BF16)
    PW = alloc("PW", [128, 4, 128], BF16)
    ov0 = cur[0]
    a = [ov0]
    AT = alloc("AT", [128, 16, NT], BF16, a)
    WS = [alloc("WS%d" % i, [128, 4096], BF16, a) for i in range(4)]
    RL = alloc("RL", [128, 2, 512], F32, a)
    mlp_end = a[0]
    a = [ov0]

    def mkset(i):
        d = {}
        d["i"] = i
        d["ALOW"] = alloc("ALOW%d" % i, [16, GM], BF16, a)
        d["VT"] = alloc("VT%d" % i, [128, 4, GM], BF16, a)
        d["SG"] = alloc("SG%d" % i, [128, 4, GM], F32, a)
        d["UT"] = alloc("UT%d" % i, [128, 4, GM + 15], F32, a)
        d["EQ"] = alloc("EQ%d" % i, [128, 2, GM], F32, a)
        d["EK"] = alloc("EK%d" % i, [128, 2, GM], F32, a)
        d["QT"] = alloc("QT%d" % i, [128, 2, GM], BF16, a)
        d["KT"] = alloc("KT%d" % i, [128, 2, GM], BF16, a)
        return d

    SETS = [mkset(0), mkset(1)]
    SS = {"i": "s"}
    SS["ALOW"] = alloc("ALOWs", [16, NS], BF16, a)
    SS["VT32"] = alloc("VT32s", [128, 4, NS], F32, a)
    SS["SG"] = alloc("SGs", [128, 4, NS], F32, a)
    SS["UT"] = alloc("UTs", [128, 4, NS + 15], F32, a)
    SS["K32"] = alloc("K32s", [128, 2, NS], F32, a)
    SS["EQ"] = alloc("EQs", [128, 2, NS], F32, a)
    SS["EK"] = alloc("EKs", [128, 2, NS], F32, a)
    SS["QT32"] = alloc("QT32s", [128, 2, NS], F32, a)
    SS["KT32"] = alloc("KT32s", [128, 2, NS], F32, a)
    PDs = alloc("PDs", [128, 4, NS], BF16, a)
    OTs = alloc("OTs", [128, 4, NS], F32, a)
    OIN = alloc("OIN", [128, 4, NS], F32, a)
    OSQs = alloc("OSQs", [128, 2, NS], BF16, a)
    RRs = alloc("RRs", [128, 2, NS], F32, a)
    MIXS = alloc("MIXS", [128, 8, NS], BF16, a)
    SQR3 = alloc("SQR3", [128, 2, NS], BF16, a)
    RT3 = alloc("RT3", [128, NS], F32, a)
    SQR1 = alloc("SQR1", [128, 2, GM], BF16, a)
    RT1 = alloc("RT1", [128, GM], F32, a)
    SQR2 = alloc("SQR2", [128, 2, 512], BF16, a)
    RT2 = alloc("RT2", [128, 512], F32, a)
    PT = alloc("PT", [128, 2, GM + 15], F32, a)
    PFX = alloc("PFX", [128, 16], F32, a)
    PD = alloc("PD", [128, 4, GM], BF16, a)
    LTOK = alloc("LTOK", [128, 2, 256], F32, a)
    LTOKB = alloc("LTOKB", [128, 2, 256], BF16, a)
    S0BF = alloc("S0BF", [128, 8, 2, 128], BF16, a)
    QTB = alloc("QTB", [128, 2, NS], BF16, a)
    KE = alloc("KE", [128, 2, GM], BF16, a)
    PROD = alloc("PROD", [128, 2, NS], F32, a)
    TMPS = alloc("TMPS", [128, 4, NS], F32, a)
    TOK = alloc("TOK", [128, 2, 768], BF16, a)
    ATTM = alloc("ATTM", [128, 2, 512], BF16, a)
    OT = alloc("OT", [128, 4, GM], F32, a)
    OSQ = alloc("OSQ", [128, 2, GM], BF16, a)
    RR = alloc("RR", [128, 2, GM], F32, a)
    MIXT = alloc("MIXT", [128, 8, 512], BF16, a)
    UB = alloc("UB", [128, 4, GM], BF16, a)
    S0_ = alloc("S0", [128, 8, 2, 128], F32, a)
    S0 = [S0_, S0_]
    XPOOL = alloc("XPOOL", [120, 512], F32, a)
    KMASK = alloc("KMASK", [16, 2, 256], BF16, a)
    TOKS = alloc("TOKS", [16, 768], BF16, a)
    mix_end = a[0]
    assert max(mlp_end, mix_end) <= SB_HI, (mlp_end, mix_end, SB_HI)

    NPS = 7
    P = [nc.alloc_psum_tensor("ps%d" % i, [128, 512], F32) for i in range(NPS)]
    PB = [nc.alloc_psum_tensor("pb%d" % i, [128, 1024], BF16) for i in range(1)]
    rr_ps = [0]
    rr_pb = [0]

    open_banks = set()

    def bank():
        for _ in range(NPS):
            i = rr_ps[0]
            rr_ps[0] = (i + 1) % NPS
            if ("ps", i) not in open_banks:
                return i
        raise RuntimeError("all PSUM banks are open")

    def bbank():
        for _ in range(1):
            i = rr_pb[0]
            rr_pb[0] = 0
            if ("pb", i) not in open_banks:
                return i
        raise RuntimeError("all bf16 PSUM banks are open")

    OVL = "OVL"

    def A(q, name, reads, writes, *args, **kw):
        for r in reads:
            open_banks.discard(r)
        return S.add(q, lambda e: getattr(e, name)(*args, **kw), reads=reads, writes=writes)

    def MM(out, lhsT, rhs, start, stop, reads, bk):
        open_banks.add(bk)
        return S.add("pe", lambda e: e.matmul(out, lhsT=lhsT, rhs=rhs, start=start, stop=stop),
                     reads=reads, writes=[bk])

    def TR(out, in_, ident, reads, bk):
        open_banks.add(bk)
        return S.add("pe", lambda e: e.transpose(out, in_, ident), reads=reads, writes=[bk])

    def DMA(q, out, in_, reads, writes, slot, cont=False):
        return S.add(q, lambda e: e.dma_start(out=out, in_=in_), reads=reads, writes=writes,
                     slot=slot, cont=cont)

    ident32 = CON[:, C_ID:C_ID + 128]
    triU = CON[:, C_TRI:C_TRI + 128]
    ones32 = CON[:, C_ONE:C_ONE + 128]
    invc = CON[:, C_INVC:C_INVC + 15]
    eps_ap = CON[:, C_EPS:C_EPS + 1]

    DMA("sp", CON[:], con_d, [], ["CON"], "con")
    DMA("sp", PV[:], pvec_d, [], ["PV"], "pv")
    A("dve", "tensor_copy", ["CON"], ["IDB"], out=IDB[:], in_=ident32)
    A("dve", "tensor_copy", ["CON"], ["ONB"], out=ONB[:], in_=ones32)
    A("dve", "tensor_copy", ["CON"], ["TRIB"], out=TRIB[:], in_=triU)

    def load_layer_weights(l):
        wv = win_d[l].rearrange("(k p) n -> p k n", p=128)
        DMA("pool", PW[:], poolw_d[l].rearrange("g c d -> c g d"), [], ["PW"], "pw")
        DMA("pool", WG[:], wgate_d[l], [], ["WG"], "wg")
        DMA("pool", BG[:], bgate_d[l:l + 1, :], [], ["BG"], "bg")
        DMA("pool", Win[:, :, 0:1032], wv[:, :, 0:1032], [], ["Win"], "win")
        DMA("pool", Win[:, :, 1032:IN_W], wv[:, :, 1032:IN_W], [], ["Win"], "win", cont=True)

    def load_wout(l):
        DMA("pool", Wout[:], wout_d[l].rearrange("(k p) n -> p k n", p=128), [], ["Wout"], "wout")

    def norm(cols, n, xkeys, gcol, out_fn, which):
        SQR, RT = {1: (SQR1, RT1), 2: (SQR2, RT2), 3: (SQR3, RT3)}[which]
        sk, rk = "SQR%d" % which, "RT%d" % which
        bk = bank()
        for c in range(8):
            sb = c % 2
            A("act", "activation", xkeys + [OVL], [(sk, sb)], out=SQR[:, sb, 0:n], in_=X[:, c, cols], func=AF.Square)
            MM(P[bk][:, 0:n], ONB[:], SQR[:, sb, 0:n], c == 0, c == 7, [(sk, sb), "ONB", OVL], ("ps", bk))
            if c % 4 == 3:
                yield
        A("act", "activation", [("ps", bk), "CON", OVL], [rk], out=RT[:, 0:n], in_=P[bk][:, 0:n], func=AF.Ln,
          scale=1.0 / D, bias=eps_ap)
        A("act", "activation", [rk, OVL], [rk], out=RT[:, 0:n], in_=RT[:, 0:n], func=AF.Exp, scale=-0.5)
        yield
        for c in range(8):
            o, ok, post = out_fn(c)
            A("dve", "scalar_tensor_tensor", xkeys + [rk, "PV", OVL], ok, out=o, in0=X[:, c, cols],
              scalar=PV[:, gcol + c:gcol + c + 1], in1=RT[:, 0:n], op0=ALU.mult, op1=ALU.mult)
            if post is not None:
                post()
            if c % 4 == 3:
                yield

    def part1(l, g, T):
        n, ts, nt = g.n, g.ts, g.nt
        si = T["i"]
        pv = l * PV_L
        hk = ("H", g.key)
        smp = g.sample
        K = lambda nm, *r: (nm, si) + r
        yield from norm(g.cols, g.n, [("X", g.key)], pv + PV_G1, lambda c: (H[:, c, g.cols], [hk], None), 1)

        pend = []

        def proj(col0, M, evac):
            bk = bank()
            for kc in range(8):
                MM(P[bk][0:M, 0:n], Win[:, kc, col0:col0 + M], H[:, kc, g.cols], kc == 0, kc == 7,
                   ["Win", hk], ("ps", bk))
            flush()
            pend.append(lambda: evac(bk))

        def flush():
            while pend:
                pend.pop(0)()

        def proj_v(h):
            if smp:
                proj(CV + h * 128, 128, lambda bk, h=h: A("act", "copy", [("ps", bk), OVL], [K("VT32")],
                                                          out=T["VT32"][:, h, 0:n], in_=P[bk][:, 0:n]))
            else:
                proj(CV + h * 128, 128, lambda bk, h=h: A("act", "copy", [("ps", bk), OVL], [K("VT")],
                                                          out=T["VT"][:, h, 0:n], in_=P[bk][:, 0:n]))

        proj(CA, 16, lambda bk: A("dve", "tensor_copy", [("ps", bk), OVL], [K("ALOW")], out=T["ALOW"][0:16, 0:n],
                                  in_=P[bk][0:16, 0:n]))
        proj_v(0)
        yield
        flush()
        yield
        bkx = bank()
        for t in range(nt):
            o = P[bkx][0:ts, t * 256:(t + 1) * 256]
            MM(o, T["ALOW"][0:16, t * ts:(t + 1) * ts], WG[0:16, :], True, False, [K("ALOW"), "WG", OVL], ("ps", bkx))
            MM(o, ONB[0:1, 0:ts], BG[0:1, :], False, True, ["ONB", "BG"], ("ps", bkx))
        ncol = nt * 256
        lt = LTOK[0:ts, :, :].rearrange("p a b -> p (a b)")[:, 0:ncol]
        A("act", "activation", [("ps", bkx), OVL], ["LTOK"], out=lt, in_=P[bkx][0:ts, 0:ncol], func=AF.Exp, scale=-1.0)
        ltb = LTOKB[0:ts, :, :].rearrange("p a b -> p (a b)")[:, 0:ncol]
        A("act", "activation", ["LTOK", OVL], ["LTOKB"], out=ltb, in_=lt, func=AF.Ln, bias=1.0)
        for h in range(1, 4):
            proj_v(h)
            yield
        cum = IDB if smp else TRIB
        for p in range(2):
            bkb = bank()
            for t in range(nt):
                MM(P[bkb][:, t * ts:(t + 1) * ts], LTOKB[0:ts, t, p * 128:(p + 1) * 128], cum[0:ts, 0:ts], True, True,
                   ["LTOKB", "TRIB", "IDB", OVL], ("ps", bkb))
            A("act", "activation", [("ps", bkb), OVL], [K("EQ", p)], out=T["EQ"][:, p, 0:n], in_=P[bkb][:, 0:n], func=AF.Exp,
              scale=-1.0 / 16)
            A("act", "activation", [("ps", bkb), OVL], [K("EK", p)], out=T["EK"][:, p, 0:n], in_=P[bkb][:, 0:n], func=AF.Exp,
              scale=1.0 / 16)
        yield
        if not smp:
            if g.first:
                A("pool", "memset", [OVL], [K("UT")], T["UT"][:, :, 0:15], 0.0)
            else:
                A("pool", "tensor_copy", ["HALO", OVL], [K("UT")], out=T["UT"][:, :, 0:15], in_=HALO[:, l, :, :])
        for gi in range(4):
            proj(CU + gi * 128, 128, lambda bk, gi=gi: A("act", "copy", [("ps", bk), OVL], [K("UT")],
                                                         out=T["UT"][:, gi, 15:15 + n], in_=P[bk][:, 0:n]))
            yield
        flush()
        if not smp:
            A("pool", "tensor_copy", [K("UT"), OVL], ["HALO"], out=HALO[:, l, :, :], in_=T["UT"][:, :, n:n + 15])
        qo, ko = (T["QT32"], T["KT32"]) if smp else (T["QT"], T["KT"])
        for p in range(2):
            proj(CQ + p * 128, 128, lambda bk, p=p: A("dve", "scalar_tensor_tensor", [("ps", bk), K("EQ", p), OVL], [K("QT")],
                                                      out=qo[:, p, 0:n], in0=P[bk][:, 0:n], scalar=0.125,
                                                      in1=T["EQ"][:, p, 0:n], op0=ALU.mult, op1=ALU.mult))
            yield
            if smp:
                flush()
                A("dve", "tensor_copy", [K("QT"), OVL], [K("QTB")], out=QTB[:, p, 0:n], in_=qo[:, p, 0:n])
                def ev(bk, p=p):
                    A("dve", "tensor_copy", [("ps", bk), OVL], [K("K32", p)], out=T["K32"][:, p, 0:n], in_=P[bk][:, 0:n])
                    A("dve", "tensor_tensor", [K("K32", p), K("EK", p), OVL], [K("KT")], out=ko[:, p, 0:n],
                      in0=T["K32"][:, p, 0:n], in1=T["EK"][:, p, 0:n], op=ALU.mult)
                proj(CK + p * 128, 128, ev)
            else:
                proj(CK + p * 128, 128, lambda bk, p=p: A("dve", "tensor_tensor", [("ps", bk), K("EK", p), OVL], [K("KT")],
                                                          out=ko[:, p, 0:n], in0=P[bk][:, 0:n], in1=T["EK"][:, p, 0:n],
                                                          op=ALU.mult))
            yield
        for h in range(4):
            proj(CG + h * 128, 128, lambda bk, h=h: A("act", "activation", [("ps", bk), OVL], [K("SG")],
                                                      out=T["SG"][:, h, 0:n], in_=P[bk][:, 0:n], func=AF.Silu))
            yield
        flush()
        yield

    def head_norm(l, g, T, mc, B):
        n = g.n
        si = T["i"]
        pv = l * PV_L
        tg = B["tag"]
        fine = B.get("fine", False)
        OT_, OSQ_, RR_, MX_ = B["OT"], B["OSQ"], B["RR"], B["MIXT"]
        K = lambda nm, *r: (nm, si) + r

        def front(h):
            sb = h % 2
            A("act", "activation", [("OT", tg), OVL], [("OSQ", tg, sb)], out=OSQ_[:, sb, 0:n], in_=OT_[:, h, 0:n], func=AF.Square)
            bk = bank()
            MM(P[bk][:, 0:n], ONB[:], OSQ_[:, sb, 0:n], True, True, [("OSQ", tg, sb), "ONB", OVL], ("ps", bk))
            return bk

        def back(h, bk):
            sb = h % 2
            A("act", "activation", [("ps", bk), "CON", OVL], [("RR", tg, sb)], out=RR_[:, sb, 0:n], in_=P[bk][:, 0:n], func=AF.Ln,
              scale=1.0 / 128, bias=eps_ap)
            A("act", "activation", [("RR", tg, sb), OVL], [("RR", tg, sb)], out=RR_[:, sb, 0:n], in_=RR_[:, sb, 0:n], func=AF.Exp,
              scale=-0.5)
            A("dve", "scalar_tensor_tensor", [("OT", tg), ("RR", tg, sb), "PV", OVL], [("RR", tg, sb)], out=RR_[:, sb, 0:n],
              in0=OT_[:, h, 0:n], scalar=PV[:, pv + PV_GG:pv + PV_GG + 1], in1=RR_[:, sb, 0:n], op0=ALU.mult, op1=ALU.mult)
            A("dve", "tensor_tensor", [("RR", tg, sb), K("SG"), OVL], [("MIXT", tg, mc)], out=MX_[:, h, mc:mc + n], in0=RR_[:, sb, 0:n],
              in1=T["SG"][:, h, 0:n], op=ALU.mult)

        prev = None
        for h in range(4):
            bk = front(h)
            if fine:
                yield
            if prev is not None:
                back(*prev)
            prev = (h, bk)
            yield
            if fine:
                yield
        back(*prev)
        yield

    def part2a(l, g, T, mc, B):
        n = g.n
        pv = l * PV_L
        smp = g.sample
        if smp:
            yield from gla_sample(l, g, T, B)
            yield from pool_sample(l, g, T, B)
        else:
            pool_prompt(l, g, T)
            yield
            yield from gla_prompt(l, g, T)
        assert smp or not mixt_busy[0], "MIXT would be overwritten before the previous pair's out-projection was emitted"
        yield from head_norm(l, g, T, mc, B)
        for gi in range(4):
            bk = bank()
            if smp:
                MM(P[bk][:, 0:n], PW[:, gi, :], B["PD"][:, gi, 0:n], True, True, ["PW", ("PD", B["tag"]), OVL], ("ps", bk))
            else:
                MM(P[bk][:, 0:n], PWS[:, gi, :], PD[:, gi, 0:n], True, False, ["PWS", ("PD", "p"), OVL], ("ps", bk))
                MM(P[bk][:, 0:n], PWN[:, gi, :], UB[:, gi, 0:n], False, True, ["PWS", "UB", OVL], ("ps", bk))
            if B.get("fine", False):
                yield
                yield
            A("act", "activation", [("ps", bk), "PV", OVL], [("MIXT", B["tag"], mc)], out=B["MIXT"][:, 4 + gi, mc:mc + n], in_=P[bk][:, 0:n],
              func=AF.Copy, scale=PV[:, pv + PV_PS + gi:pv + PV_PS + gi + 1])
        yield

    mixt_busy = [False]

    def part2b(l, keys, cols, n, mcs, B, which):
        pv = l * PV_L
        MX_ = B["MIXT"]
        if B["tag"] == "p":
            mixt_busy[0] = True
        xks = [("X", k) for k in keys]
        hks = [("H", k) for k in keys]
        mks = [("MIXT", B["tag"], m) for m in mcs]
        prev = None

        def add(dc, bk):
            A("dve", "tensor_tensor", [("ps", bk)] + xks, xks, out=X[:, dc, cols], in0=P[bk][:, 0:n], in1=X[:, dc, cols],
              op=ALU.add)

        for dc in range(8):
            bk = bank()
            for kc in range(8):
                MM(P[bk][:, 0:n], Wout[:, kc, dc * 128:(dc + 1) * 128], MX_[:, kc, 0:n], kc == 0, kc == 7,
                   ["Wout", OVL] + mks, ("ps", bk))
            if B.get("fine", False):
                yield
                yield
                add(dc, bk)
            else:
                if prev is not None:
                    add(*prev)
                prev = (dc, bk)
            if dc == 7 and B["tag"] == "p":
                mixt_busy[0] = False
            yield
        if prev is not None:
            add(*prev)
            yield
        yield from norm(cols, n, xks, pv + PV_G2, lambda c: (H[:, c, cols], hks, None), which)

    sbf_cur = [0]

    def gla_prompt(l, g, T):
        si = T["i"]
        K = lambda nm, *r: (nm, si) + r
        EQ, KT, QT, VT = T["EQ"], T["KT"], T["QT"], T["VT"]
        if g.first:
            A("dve", "memset", [], [("SST", l)], SST[:, l, :, :], 0.0)
        if g.key == 0:
            A("act", "copy", [("SST", l)], [("SBF", sbf_cur[0])], out=SBF[:, sbf_cur[0], :, :], in_=SST[:, l, :, :])
        for t in range(g.nt):
            tc_ = slice(t * 128, (t + 1) * 128)
            last = t * 128 + 127
            sc = sbf_cur[0]
            sn = 1 - sc
            sbf_cur[0] = sn
            tb = t % 2
            for p in range(2):
                A("dve", "tensor_scalar", [K("KT"), K("EQ", p), OVL], ["KE"], out=KE[:, p, tc_], in0=KT[:, p, tc_],
                  scalar1=EQ[:, p, last:last + 1], scalar2=None, op0=ALU.mult)
            bab = [bank(), bank()]
            for h in range(4):
                p, hh, r = h // 2, h % 2, slice((h % 2) * 64, (h % 2) * 64 + 64)
                MM(P[bab[hh]][:, p * 128:(p + 1) * 128], KT[r, p, tc_], QT[r, p, tc_], True, True, [K("KT"), K("QT"), OVL],
                   ("ps", bab[hh]))
            yield
            pb = bbank()
            for p in range(2):
                TR(PB[pb][:, p * 128:(p + 1) * 128], KE[:, p, tc_], IDB[:], ["KE", "IDB", OVL], ("pb", pb))
            for h in range(4):
                TR(PB[pb][:, 256 + h * 128:256 + (h + 1) * 128], VT[:, h, tc_], IDB[:], [K("VT"), "IDB", OVL], ("pb", pb))
            for hh in range(2):
                A("dve", "tensor_tensor", [("ps", bab[hh]), "CON", OVL], [("ATTM", tb)],
                  out=ATTM[:, tb, hh * 256:(hh + 1) * 256].rearrange("p (h c) -> p h c", h=2),
                  in0=P[bab[hh]][:, 0:256].rearrange("p (h c) -> p h c", h=2),
                  in1=triU.unsqueeze(1).to_broadcast([128, 2, 128]), op=ALU.mult)
            yield
            A("act", "copy", [("pb", pb), OVL], [("TOK", tb)], out=TOK[:, tb, :], in_=PB[pb][:, 0:768])
            yield
            bd = bank()
            for p in range(2):
                MM(P[bd][:, p * 256:(p + 1) * 256], TOK[:, tb, p * 128:(p + 1) * 128],
                   TOK[:, tb, 256 + 2 * p * 128:256 + (2 * p + 2) * 128], True, True, [("TOK", tb), OVL], ("ps", bd))
            bo = bank()
            for h in range(4):
                p, r = h // 2, slice((h % 2) * 64, (h % 2) * 64 + 64)
                o = P[bo][:, h * 128:(h + 1) * 128]
                ai = (h % 2) * 2 + h // 2
                MM(o, TOK[:, tb, 256 + h * 128:256 + (h + 1) * 128], ATTM[:, tb, ai * 128:(ai + 1) * 128], True, False,
                   [("TOK", tb), ("ATTM", tb), OVL], ("ps", bo))
                MM(o, SBF[r, sc, p, :], QT[r, p, tc_], False, True, [("SBF", sc), K("QT"), OVL], ("ps", bo))
            yield
            for p in range(2):
                for hh in range(2):
                    r = slice(hh * 64, hh * 64 + 64)
                    A("dve", "scalar_tensor_tensor", [("ps", bd), K("EQ", p), ("SST", l), OVL], [("SST", l)],
                      out=SST[r, l, p, :], in0=SST[r, l, p, :], scalar=EQ[r, p, last:last + 1],
                      in1=P[bd][r, p * 256 + hh * 128:p * 256 + (hh + 1) * 128], op0=ALU.mult, op1=ALU.add)
            A("act", "copy", [("ps", bo), OVL], [("OT", "p")], out=OT[:, :, tc_],
              in_=P[bo][:, :].rearrange("p (h c) -> p h c", h=4))
            yield
            A("act", "copy", [("SST", l), OVL], [("SBF", sn)], out=SBF[:, sn, :, :], in_=SST[:, l, :, :])
            yield

    def gla_sample(l, g, T, B):
        n = NS
        si = T["i"]
        K = lambda nm, *r: (nm, si) + r
        EQ, QT32, KT32, VT32, K32 = T["EQ"], T["QT32"], T["KT32"], T["VT32"], T["K32"]
        S0t = S0[0]

        def load(bh):
            for p in range(2):
                src = sgla_d[l, bh * 8:(bh + 1) * 8].rearrange("b h k v -> (h k) b v")[p * 128:(p + 1) * 128]
                DMA("sp", S0t[:, :, p, :], src, [OVL], [("S0", 0)], "s0_0", cont=(p == 1))

        load(0)
        yield
        bt = bank()
        for p in range(2):
            TR(P[bt][0:NS, p * 128:(p + 1) * 128], K32[:, p, 0:n], ident32, [K("K32", p), "CON", OVL], ("ps", bt))
        yield
        A("dve", "tensor_copy", [("ps", bt), OVL], ["TOKS"], out=TOKS[0:NS, 0:256], in_=P[bt][0:NS, 0:256])
        bt = bank()
        for h in range(4):
            TR(P[bt][0:NS, h * 128:(h + 1) * 128], VT32[:, h, 0:n], ident32, [K("VT32"), "CON", OVL], ("ps", bt))
        yield
        A("dve", "tensor_copy", [("ps", bt), OVL], ["TOKS"], out=TOKS[0:NS, 256:768], in_=P[bt][0:NS, 0:512])
        A("dve", "tensor_tensor", [K("QT"), K("KT"), OVL], ["PROD"], out=PROD[:, :, :], in0=QT32[:, :, :], in1=KT32[:, :, :],
          op=ALU.mult)
        yield
        bqb = [bank(), bank()]
        for h in range(4):
            p, hh, r = h // 2, h % 2, slice((h % 2) * 64, (h % 2) * 64 + 64)
            MM(P[bqb[hh]][:, p * NS:(p + 1) * NS], ones32[r, :], PROD[r, p, :], True, True, ["CON", "PROD", OVL],
               ("ps", bqb[hh]))
        yield
        for h in range(4):
            p, hh = h // 2, h % 2
            A("dve", "tensor_tensor", [("ps", bqb[hh]), K("VT32"), OVL], ["TMPS"], out=TMPS[:, h, :],
              in0=P[bqb[hh]][:, p * NS:(p + 1) * NS], in1=VT32[:, h, :], op=ALU.mult)
        yield
        for bh in range(2):
            if bh == 1:
                load(1)
                yield
                yield
                yield
            A("act", "copy", [("S0", 0), OVL], ["S0BF"], out=S0BF[:, :, :, :], in_=S0t[:, :, :, :])
            yield
            yield
            bob = [bank(), bank()]
            for bl in range(8):
                b = bh * 8 + bl
                for h in range(4):
                    p, hh, r = h // 2, h % 2, slice((h % 2) * 64, (h % 2) * 64 + 64)
                    MM(P[bob[hh]][:, p * NS + b:p * NS + b + 1], S0BF[r, bl, p, :], QTB[r, p, b:b + 1], True, True,
                       ["S0BF", K("QTB"), OVL], ("ps", bob[hh]))
            yield
            yield
            for h in range(4):
                p, hh = h // 2, h % 2
                A("act", "copy", [("ps", bob[hh]), OVL], ["OIN"], out=OIN[:, h, bh * 8:(bh + 1) * 8],
                  in_=P[bob[hh]][:, p * NS + bh * 8:p * NS + bh * 8 + 8])
            yield
            for bq in range(4):
                A("dve", "tensor_tensor", ["TOKS", "IDB", OVL], ["KMASK"], out=KMASK[0:NS, :, :],
                  in0=TOKS[0:NS, 0:256].unsqueeze(1).to_broadcast([NS, 2, 256]),
                  in1=IDB[0:NS, bh * 8 + bq * 2:bh * 8 + bq * 2 + 2].unsqueeze(2).to_broadcast([NS, 2, 256]), op=ALU.mult)
                yield
                bds = []
                for bl2 in range(2):
                    bd = bank()
                    bds.append(bd)
                    for p in range(2):
                        MM(P[bd][:, p * 256:(p + 1) * 256], KMASK[0:NS, bl2, p * 128:(p + 1) * 128],
                           TOKS[0:NS, 256 + 2 * p * 128:256 + (2 * p + 2) * 128], True, True, ["KMASK", "TOKS", OVL], ("ps", bd))
                yield
                yield
                for bl2 in range(2):
                    bl = bq * 2 + bl2
                    b = bh * 8 + bl
                    bd = bds[bl2]
                    for p in range(2):
                        for hh in range(2):
                            r = slice(hh * 64, hh * 64 + 64)
                            A("dve", "scalar_tensor_tensor", [("ps", bd), K("EQ", p), ("S0", 0), OVL], [("S0", 0)],
                              out=S0t[r, bl, p, :], in0=S0t[r, bl, p, :], scalar=EQ[r, p, b:b + 1],
                              in1=P[bd][r, p * 256 + hh * 128:p * 256 + (hh + 1) * 128], op0=ALU.mult, op1=ALU.add)
                    yield
            DMA("sp", glas_d[l][:, bh * 8:(bh + 1) * 8, :, :], S0t[:], [("S0", 0), OVL], [], "s0o_0")
            yield
        for h in range(4):
            A("dve", "tensor_tensor", ["OIN", "TMPS", OVL], [("OT", B["tag"])], out=B["OT"][:, h, 0:n],
              in0=OIN[:, h, :], in1=TMPS[:, h, :], op=ALU.add)
        yield

    def pool_prompt(l, g, T):
        n = g.n
        L = n + 15
        si = T["i"]
        UT = T["UT"]
        uk = ("UT", si)
        A("dve", "tensor_copy", [uk, OVL], ["UB"], out=UB[:, :, 0:n], in_=UT[:, :, 15:L])
        for gi, w in enumerate(POOL_W):
            src = UT[:, gi, :]
            steps = []
            sh = 1
            while sh < w:
                steps.append(sh)
                sh *= 2
            starts = [15]
            for s_ in reversed(steps):
                starts.append(starts[-1] - s_)
            starts = list(reversed(starts))
            cur_ap = src
            for i, s_ in enumerate(steps):
                a0 = starts[i + 1]
                lastlvl = (i == len(steps) - 1)
                if lastlvl and not g.first:
                    A("pool", "tensor_tensor", [uk, ("PT", 0), ("PT", 1), OVL], [("PD", "p")], out=PD[:, gi, 0:n],
                      in0=cur_ap[:, 15:L], in1=cur_ap[:, 15 - s_:L - s_], op=ALU.add)
                else:
                    dst = PT[:, i % 2, :]
                    A("pool", "tensor_tensor", [uk, ("PT", 0), ("PT", 1), OVL], [("PT", i % 2)], out=dst[:, a0:L],
                      in0=cur_ap[:, a0:L], in1=cur_ap[:, a0 - s_:L - s_], op=ALU.add)
                    cur_ap = dst
            if g.first:
                k = w - 1
                A("pool", "tensor_copy", [("PT", 0), ("PT", 1), OVL], [("PD", "p")], out=PD[:, gi, 0:n], in_=cur_ap[:, 15:L])
                A("pool", "tensor_tensor", [("PT", 0), ("PT", 1), "CON", OVL], [("PD", "p")], out=PD[:, gi, 0:k],
                  in0=cur_ap[:, 15:15 + k], in1=CON[:, C_WIC + gi * 15:C_WIC + gi * 15 + k], op=ALU.mult)

    def pool_sample(l, g, T, B):
        n = NS
        si = T["i"]
        UT = T["UT"]
        uk = ("UT", si)
        DMA("sp", pools_old_d[l], spool_d[l, :, 1:15, :], [], [], "psold")
        DMA("sp", pools_new_d[l], UT[:, :, 15:15 + n], [uk, OVL], [], "psnew")
        bk = None
        for bh in range(2):
            DMA("sp", XPOOL[:], spool_d[l, bh * 8:(bh + 1) * 8].rearrange("b r c -> (b r) c"), [OVL], ["XPOOL"], "xpool")
            yield
            yield
            if bk is None:
                bk = bank()
            for gi in range(4):
                MM(P[bk][:, gi * NS + bh * 8:gi * NS + bh * 8 + 8], XPOOL[:, gi * 128:(gi + 1) * 128],
                   CON[0:120, C_WSEL + gi * 8:C_WSEL + gi * 8 + 8], True, True, ["XPOOL", "CON", OVL], ("ps", bk))
            yield
        yield
        for gi, w in enumerate(POOL_W):
            A("dve", "scalar_tensor_tensor", [("ps", bk), uk, OVL], [("PD", B["tag"])], out=B["PD"][:, gi, 0:n], in0=UT[:, gi, 15:15 + n],
              scalar=1.0 / w - 1.0, in1=P[bk][:, gi * NS:(gi + 1) * NS], op0=ALU.mult, op1=ALU.add)
        yield

    def mlp(l, groups, nxt, prefetch_only=False):
        blocks = []
        for fh in range(2):
            for fb in range(4):
                blocks.append(("up", fh, fb))
            for db in range(4):
                blocks.append(("dn", fh, db))

        def bufof(i):
            return (WSX, ("WSX", 0), "wsx") if i == 0 else (WS[(i - 1) % 4], ("WS", (i - 1) % 4), "ws%d" % ((i - 1) % 4))

        def issue(i):
            kind, fh, j = blocks[i]
            wt, wkey, slot = bufof(i)
            if kind == "up":
                c0 = fh * 2048 + j * 512
                src = wup_d[l].rearrange("(k p) f -> p k f", p=128)[:, :, c0:c0 + 512]
                dst = wt[:, :].rearrange("p (k f) -> p k f", k=8)
            else:
                src = wdn_d[l, fh * 2048:(fh + 1) * 2048, j * 256:(j + 1) * 256].rearrange("(c p) d -> p c d", p=128)
                dst = wt[:, :].rearrange("p (c d) -> p c d", c=16)
            DMA("pool", dst, src, [] if i == 0 else [OVL], [wkey], slot)

        if prefetch_only:
            issue(0)
            return
        for i in range(1, 4):
            issue(i)
        rlb = [0]
        for i, (kind, fh, j) in enumerate(blocks):
            if i >= 1 and i + 3 < len(blocks):
                issue(i + 3)
            if i == 11 and nxt is not None:
                load_layer_weights(nxt)
            wt, wkey, _ = bufof(i)
            if kind == "up":
                wv = wt[:, :].rearrange("p (k f) -> p k f", k=8)
                for jj in range(4):
                    fc = j * 4 + jj
                    for keys, cols, n in groups:
                        bk = bank()
                        hks = [("H", k) for k in keys]
                        for kc in range(8):
                            MM(P[bk][:, 0:n], wv[:, kc, jj * 128:(jj + 1) * 128], H[:, kc, cols], kc == 0, kc == 7,
                               [wkey] + hks + [OVL], ("ps", bk))
                        rb = rlb[0]
                        rlb[0] = 1 - rb
                        A("act", "activation", [("ps", bk), OVL], [("RL", rb)], out=RL[:, rb, 0:n], in_=P[bk][:, 0:n],
                          func=AF.Relu)
                        A("dve", "tensor_tensor", [("RL", rb), OVL], [("AT", keys[0])], out=AT[:, fc, cols], in0=RL[:, rb, 0:n],
                          in1=RL[:, rb, 0:n], op=ALU.mult)
            else:
                wv = wt[:, :].rearrange("p (c d) -> p c d", c=16)
                for dl in range(2):
                    dc = j * 2 + dl
                    for keys, cols, n in groups:
                        bk = bank()
                        xks = [("X", k) for k in keys]
                        for fc in range(16):
                            MM(P[bk][:, 0:n], wv[:, fc, dl * 128:(dl + 1) * 128], AT[:, fc, cols], fc == 0, fc == 15,
                               [wkey, ("AT", keys[0]), OVL], ("ps", bk))
                        A("dve", "tensor_tensor", [("ps", bk)] + xks, xks, out=X[:, dc, cols], in0=P[bk][:, 0:n],
                          in1=X[:, dc, cols], op=ALU.add)

    def phase_switch():
        A("dve", "memset", [], [OVL], RL[:, 0, 0:8], 0.0)

    STAT = {'rounds': 0, 'bgsteps': 0, 'bgtail': 0}
    order = [(sg, l) for sg in range(nsg) for l in range(depth)]
    load_layer_weights(0)
    for sg in range(nsg):
        col0 = sg * 1024
        ncols = 1024 if sg == 0 else NT
        grps = [Grp(i, i * GM, GM, False, col0 + i * GM, sg == 0 and i == 0) for i in range(4)]
        if sg == 1:
            grps.append(Grp("s", 1024, NS, True, 2048, False))
        xkeys = [("X", g.key) for g in grps]
        xsrc = xT_d.rearrange("(c p) t -> p c t", p=128)
        DMA("sp", X[:, :, 0:512], xsrc[:, :, col0:col0 + 512], [], [("X", 0), ("X", 1)], "x0")
        DMA("sp", X[:, :, 512:1024], xsrc[:, :, col0 + 512:col0 + 1024], [], [("X", 2), ("X", 3)], "x1")
        if sg == 1:
            DMA("sp", X[:, :, 1024:NT], xsrc[:, :, 2048:NTOK], [], [("X", "s")], "x2")
        mgroups = [([0, 1], slice(0, 512), 512), ([2, 3], slice(512, 1024), 512)]
        if sg == 1:
            mgroups = [([0, 1], slice(0, 347), 347), ([1, 2], slice(347, 694), 347), ([2, 3, "s"], slice(694, 1040), 346)]
        for l in range(depth):
            idx = order.index((sg, l))
            nxt = order[idx + 1][1] if idx + 1 < len(order) else None
            for gi, w in enumerate(POOL_W):
                A("act", "activation", ["PW"], ["PWS"], out=PWS[:, gi, :], in_=PW[:, gi, :], func=AF.Copy, scale=1.0 / w)
            A("act", "activation", ["PW"], ["PWS"], out=PWN[:, :, :], in_=PW[:, :, :], func=AF.Copy, scale=-1.0)
            def rr_run(iters, bg=None):
                live = [x if isinstance(x, tuple) else (x, 1) for x in iters]
                while live:
                    for ent in list(live):
                        it, k = ent
                        for _ in range(k):
                            try:
                                next(it)
                            except StopIteration:
                                live.remove(ent)
                                break
                    STAT["rounds"] += 1
                    for _ in range(1):
                        if bg is not None and bg[0] is not None:
                            try:
                                next(bg[0])
                                STAT["bgsteps"] += 1
                            except StopIteration:
                                bg[0] = None

            PB2 = dict(OT=OT, OSQ=OSQ, RR=RR, PD=PD, MIXT=MIXT, tag="p")
            SB2 = dict(OT=OTs, OSQ=OSQs, RR=RRs, PD=PDs, MIXT=MIXS, tag="s", fine=True)
            pairs = [([0, 1], slice(0, 512), 512, [0, GM], PB2, 2), ([2, 3], slice(512, 1024), 512, [0, GM], PB2, 2)]
            pg = [g for g in grps if not g.sample]
            bg = [None]
            if sg == 1:
                gs = grps[-1]
                rr_run([part1(l, gs, SS)])

                def bg_gen(l=l, gs=gs):
                    yield from part2a(l, gs, SS, 0, SB2)
                    yield from part2b(l, ["s"], slice(1024, 1040), NS, [0], SB2, 3)

                bg[0] = bg_gen()
            load_wout(l)
            its1 = [part1(l, g, SETS[i % 2]) for i, g in enumerate(pg)]
            rr_run([its1[0]], bg)
            mlp(l, mgroups, None, prefetch_only=True)
            pending_b = None
            for i, g in enumerate(pg):
                mc = 0 if i % 2 == 0 else GM
                iters = [part2a(l, g, SETS[i % 2], mc, PB2)]
                if i + 1 < len(pg):
                    iters.append(its1[i + 1])
                if pending_b is not None:
                    iters.insert(0, (part2b(l, *pending_b), 2))
                    pending_b = None
                rr_run(iters, bg)
                if i % 2 == 1:
                    pending_b = pairs[i // 2]
            rr_run([part2b(l, *pending_b)], bg)
            while bg[0] is not None:
                rr_run([], bg) if False else None
                try:
                    next(bg[0])
                    STAT["bgtail"] += 1
                except StopIteration:
                    bg[0] = None
            if sg == 1:
                DMA("sp", glap_d[l], SST[:, l, :, :], [("SST", l)], [], "glap")
                DMA("sp", poolp_d[l], HALO[:, l, :, :], ["HALO"], [], "poolp")
            phase_switch()
            mlp(l, mgroups, nxt)
            phase_switch()
        OTS = OT[:, :, :].rearrange("p (a h) c -> p a (h c)", a=2)
        fpairs = [([0, 1], slice(0, 512), 512, col0), ([2, 3], slice(512, 1024), 512, col0 + 512)]
        if sg == 1:
            fpairs.append((["s"], slice(1024, 1040), NS, 2048))
        for keys, cols, n, gcol0 in fpairs:
            def out_fn(c, n=n, gcol0=gcol0):
                yb_ = c % 2
                def post():
                    DMA("sp", yT_d[c * 128:(c + 1) * 128, gcol0:gcol0 + n], OTS[:, yb_, 0:n], [("YS", yb_), OVL], [],
                        "yo%d" % yb_)
                return OTS[:, yb_, 0:n], [("YS", yb_), ("OT", "p")], post
            for _ in norm(cols, n, [("X", k) for k in keys], PV_FIN, out_fn, 2):
                pass
        phase_switch()

    print('n_ops', len(S.ops), STAT)
    if maxops is not None:
        S.ops = S.ops[:maxops]
    S.emit(nc, st)
    st.close()
    return nc


_NC_CACHE = {}


def _host_consts():
    con = np.zeros((128, NCON), np.float32)
    con[:, C_ID:C_ID + 128] = np.eye(128, dtype=np.float32)
    con[:, C_TRI:C_TRI + 128] = np.triu(np.ones((128, 128), np.float32))
    con[:, C_ONE:C_ONE + 128] = 1.0
    con[:, C_INVC:C_INVC + 15] = (1.0 / np.arange(1, 16, dtype=np.float64)).astype(np.float32)[None, :]
    con[:, C_EPS] = EPS
    for gi, w in enumerate(POOL_W):
        for b in range(8):
            for r in range(15 - (w - 1), 15):
                con[b * 15 + r, C_WSEL + gi * 8 + b] = 1.0 / w
    for gi, w in enumerate(POOL_W):
        con[:, C_WIC + gi * 15:C_WIC + gi * 15 + 15] = (w / np.arange(1, 16, dtype=np.float64)).astype(np.float32)[None, :]
    return con


def kernel(x_prompt, x_sample, state_gla, state_pool, norm1_g, w_in, w_gate, b_gate, gla_norm_g, pool_w,
           pool_scale, w_out, norm2_g, w_up, w_down, final_g):
    f = lambda a: np.ascontiguousarray(np.asarray(a, dtype=np.float32))
    x_prompt, x_sample, state_gla, state_pool = f(x_prompt), f(x_sample), f(state_gla), f(state_pool)
    w_in, w_out, w_up, w_down = f(w_in), f(w_out), f(w_up), f(w_down)
    pool_w, w_gate, b_gate = f(pool_w), f(w_gate), f(b_gate)
    pvec = np.zeros((128, NPV), np.float32)
    n1, n2, gg, ps, fg = f(norm1_g), f(norm2_g), f(gla_norm_g), f(pool_scale), f(final_g)
    for l in range(DEPTH):
        b = l * PV_L
        pvec[:, b + PV_G1:b + PV_G1 + 8] = n1[l].reshape(8, 128).T
        pvec[:, b + PV_G2:b + PV_G2 + 8] = n2[l].reshape(8, 128).T
        pvec[:, b + PV_GG] = gg[l]
        pvec[:, b + PV_PS:b + PV_PS + 4] = ps[l].reshape(4, 128).T
    pvec[:, PV_FIN:PV_FIN + 8] = fg.reshape(8, 128).T
    con = _host_consts()
    if "nc" not in _NC_CACHE:
        _NC_CACHE["nc"] = build_program()
    nc = _NC_CACHE["nc"]
    in_maps = []
    for c in range(NCORE):
        xT = np.empty((D, NTOK), np.float32)
        xT[:, :SEQ] = x_prompt[c].T
        xT[:, SEQ:] = x_sample[c * NS:(c + 1) * NS, 0, :].T
        in_maps.append({
            "xT": xT,
            "sgla": np.ascontiguousarray(state_gla[:, c * NS:(c + 1) * NS]),
            "spool": np.ascontiguousarray(state_pool[:, c * NS:(c + 1) * NS]),
            "w_in": w_in, "w_out": w_out, "w_up": w_up, "w_down": w_down,
            "pool_w": pool_w, "w_gate": w_gate, "b_gate": b_gate, "pvec": pvec, "consts": con,
        })
    res = run_bass_kernel_spmd(nc, in_maps, core_ids=list(range(NCORE)))
    B = NCORE
    y_prompt = np.empty((B, SEQ, D), np.float32)
    y_sample = np.empty((B * NS, 1, D), np.float32)
    gla_p = np.empty((DEPTH, B, 4, 64, 128), np.float32)
    pool_p = np.empty((DEPTH, B, 15, 512), np.float32)
    gla_s = np.empty((DEPTH, B * NS, 4, 64, 128), np.float32)
    pool_s = np.empty((DEPTH, B * NS, 15, 512), np.float32)
    for c in range(NCORE):
        r = res.results[c]
        yT = np.asarray(r["yT"])
        y_prompt[c] = yT[:, :SEQ].T
        y_sample[c * NS:(c + 1) * NS, 0, :] = yT[:, SEQ:].T
        gp = np.asarray(r["gla_p"]).reshape(DEPTH, 2, 64, 2, 128)
        gla_p[:, c] = gp.transpose(0, 3, 1, 2, 4).reshape(DEPTH, 4, 64, 128)
        pp = np.asarray(r["pool_pT"])
        pool_p[:, c] = pp.transpose(0, 3, 2, 1).reshape(DEPTH, 15, 512)
        gs = np.asarray(r["gla_s"]).reshape(DEPTH, 2, 64, NS, 2, 128)
        gla_s[:, c * NS:(c + 1) * NS] = gs.transpose(0, 3, 4, 1, 2, 5).reshape(DEPTH, NS, 4, 64, 128)
        pool_s[:, c * NS:(c + 1) * NS, 0:14] = np.asarray(r["pool_s_old"])
        pn = np.asarray(r["pool_s_newT"])
        pool_s[:, c * NS:(c + 1) * NS, 14] = pn.transpose(0, 3, 2, 1).reshape(DEPTH, NS, 512)
    return (y_prompt, y_sample, gla_p, pool_p, gla_s, pool_s)
```
